# Optimizing a Trainium2 kernel written in Bass

```python
import jax, jax.numpy as jnp
from jax import lax
import numpy as np

D_MODEL = 1024
BATCH = 8
SEQ = 4096
DEPTH = 2
DEC_BATCH = 2
DEC_SEQ = 16384
PAST_LEN = 128

HEAD_DIM = 64
N_HEADS_CONV = 8
N_HEADS_SG = 8
D_CONV = N_HEADS_CONV * HEAD_DIM
D_SG = N_HEADS_SG * HEAD_DIM
D_MIX = D_CONV + D_SG
D_IN = 3 * D_CONV + 2 * D_SG
CONV_WIDTH = 3
CHUNK = 128
D_FF = 2816
N_MOD = 9
EPS = 1e-6

kernel_name = "hybrid_conv_sgu_macaron_adaln_encoder"


def rms_norm(x, g):
    xf = x.astype(jnp.float32)
    y = xf * lax.rsqrt(jnp.mean(xf * xf, axis=-1, keepdims=True) + EPS)
    return (y * g.astype(jnp.float32)).astype(x.dtype)


def layer_norm(x, g):
    xf = x.astype(jnp.float32)
    mu = jnp.mean(xf, axis=-1, keepdims=True)
    d = xf - mu
    y = d * lax.rsqrt(jnp.mean(d * d, axis=-1, keepdims=True) + EPS)
    return (y * g.astype(jnp.float32)).astype(x.dtype)


def swiglu(h, w1, w2):
    gate, up = jnp.split(h @ w1, 2, axis=-1)
    return (jax.nn.silu(gate) * up) @ w2


def centred_depthwise_conv3(h, w):
    hp = jnp.pad(h, ((0, 0), (1, 1), (0, 0)))
    return hp[:, :-2] * w[0] + hp[:, 1:-1] * w[1] + hp[:, 2:] * w[2]


def token_mixing(h, w_in, conv_w, sg_norm_g, sg_ws, sg_bs, grp_norm_g, w_out):
    bsz, seq, _ = h.shape
    z = h @ w_in
    b_gate = z[..., :D_CONV]
    c_gate = z[..., D_CONV:2 * D_CONV]
    x_in = z[..., 2 * D_CONV:3 * D_CONV]
    uv = jax.nn.gelu(z[..., 3 * D_CONV:], approximate=False)
    y_a = b_gate * centred_depthwise_conv3(c_gate * x_in, conv_w)
    u, v = uv[..., :D_SG], uv[..., D_SG:]
    v = layer_norm(v, sg_norm_g)
    vc = v.reshape(bsz, seq // CHUNK, CHUNK, N_HEADS_SG, HEAD_DIM)
    sv = jnp.einsum('hpq,bnqhd->bnphd', sg_ws, vc) + sg_bs.T[None, None, :, :, None]
    y_b = u * sv.reshape(bsz, seq, D_SG)
    y = jnp.concatenate([rms_norm(y_a, grp_norm_g[:D_CONV]),
                         rms_norm(y_b, grp_norm_g[D_CONV:])], axis=-1)
    return y @ w_out


def trunk(x, c, ada_w, ada_b, norm_g, ffn_w1, ffn_w2, mix_w_in, conv_w,
          sg_norm_g, sg_ws, sg_bs, grp_norm_g, mix_w_out, final_g):
    bsz = x.shape[0]
    sc = jax.nn.silu(c)
    for l in range(DEPTH):
        mod = (sc @ ada_w[l] + ada_b[l]).reshape(bsz, N_MOD, 1, D_MODEL)

        def modulate(t, k):
            return rms_norm(t, norm_g[l, k]) * (1 + mod[:, 3 * k + 1]) + mod[:, 3 * k]

        x = x + 0.5 * mod[:, 2] * swiglu(modulate(x, 0), ffn_w1[l, 0], ffn_w2[l, 0])
        x = x + mod[:, 5] * token_mixing(modulate(x, 1), mix_w_in[l], conv_w[l], sg_norm_g[l],
                                         sg_ws[l], sg_bs[l], grp_norm_g[l], mix_w_out[l])
        x = x + 0.5 * mod[:, 8] * swiglu(modulate(x, 2), ffn_w1[l, 1], ffn_w2[l, 1])
    return rms_norm(x, final_g)


def setup_inputs(seed: int = 0) -> dict:
    key = jax.random.key(seed)
    ks = jax.random.split(key, 20)
    f32 = jnp.float32
    nrm = lambda k, s, sc: jax.random.normal(k, s, f32) * sc
    return {
        "x_prompt": nrm(ks[0], (BATCH, SEQ, D_MODEL), 1.0),
        "x_sample": nrm(ks[1], (DEC_BATCH, DEC_SEQ, D_MODEL), 1.0),
        "c_prompt": nrm(ks[2], (BATCH, D_MODEL), 1.0),
        "c_sample": nrm(ks[3], (DEC_BATCH, D_MODEL), 1.0),
        "ada_w": nrm(ks[4], (DEPTH, D_MODEL, N_MOD * D_MODEL), 0.5 * D_MODEL ** -0.5),
        "ada_b": nrm(ks[5], (DEPTH, N_MOD * D_MODEL), 0.01),
        "norm_g": 1.0 + nrm(ks[6], (DEPTH, 3, D_MODEL), 0.02),
        "ffn_w1": nrm(ks[7], (DEPTH, 2, D_MODEL, 2 * D_FF), D_MODEL ** -0.5),
        "ffn_w2": nrm(ks[8], (DEPTH, 2, D_FF, D_MODEL), D_FF ** -0.5),
        "mix_w_in": nrm(ks[9], (DEPTH, D_MODEL, D_IN), D_MODEL ** -0.5),
        "conv_w": nrm(ks[10], (DEPTH, CONV_WIDTH, D_CONV), CONV_WIDTH ** -0.5),
        "sg_norm_g": 1.0 + nrm(ks[11], (DEPTH, D_SG), 0.02),
        "sg_ws": nrm(ks[12], (DEPTH, N_HEADS_SG, CHUNK, CHUNK), CHUNK ** -0.5),
        "sg_bs": 1.0 + nrm(ks[13], (DEPTH, N_HEADS_SG, CHUNK), 0.01),
        "grp_norm_g": 1.0 + nrm(ks[14], (DEPTH, D_MIX), 0.02),
        "mix_w_out": nrm(ks[15], (DEPTH, D_MIX, D_MODEL), D_MIX ** -0.5),
        "final_g": 1.0 + nrm(ks[16], (D_MODEL,), 0.02),
    }


def reference(x_prompt, x_sample, c_prompt, c_sample, ada_w, ada_b, norm_g, ffn_w1, ffn_w2,
              mix_w_in, conv_w, sg_norm_g, sg_ws, sg_bs, grp_norm_g, mix_w_out, final_g):
    y_prompt = trunk(x_prompt, c_prompt, ada_w, ada_b, norm_g, ffn_w1, ffn_w2, mix_w_in, conv_w,
                     sg_norm_g, sg_ws, sg_bs, grp_norm_g, mix_w_out, final_g)
    y_sample = trunk(x_sample, c_sample, ada_w, ada_b, norm_g, ffn_w1, ffn_w2, mix_w_in, conv_w,
                     sg_norm_g, sg_ws, sg_bs, grp_norm_g, mix_w_out, final_g)
    return (y_prompt, y_sample)
```

```python
import math
from contextlib import ExitStack

import numpy as np
import concourse.bass as bass
import concourse.mybir as mybir
from concourse.bass_utils import run_bass_kernel_spmd

F32 = mybir.dt.float32
F32R = mybir.dt.float32r
BF16 = mybir.dt.bfloat16
AF = mybir.ActivationFunctionType
ALU = mybir.AluOpType

D = 1024
KC = 8
FF = 2816
JC = 22
DIN = 2560
HALO = 129
EPS = 1e-6
NSEG = 2
NT = 386
TSUB = 776
NSLOT = 4
SLOT_ELEMS = 4096
N_CORES = 8


def split_even(a, b, maxw):
    n = b - a
    assert n % 2 == 0 and n > 0
    h = n // 2
    k = -(-n // maxw)
    base, rem = divmod(h, k)
    out = []
    s = a
    for i in range(k):
        w = 2 * (base + (1 if i < rem else 0))
        if w > 0:
            out.append((s, s + w))
        s += w
    assert s == b
    return out


class Sem:
    def __init__(self, handle):
        self.h = handle
        self.count = 0


class Res:
    __slots__ = ("w", "r", "name", "rng")

    def __init__(self, name="", rng=None):
        self.w = None
        self.r = {}
        self.name = name
        self.rng = rng


class Emitter:
    ENGS = ("pe", "act", "dve", "pool", "sp")

    def __init__(self, nc, stack):
        self.nc = nc
        self.stack = stack
        self.q = {e: [] for e in self.ENGS}
        self.seen = {e: {} for e in self.ENGS}
        self.esem = {e: self.new_sem("e_" + e) for e in ("pe", "act", "dve", "pool")}
        self.arena = []

    def new_sem(self, name):
        return Sem(self.stack.enter_context(self.nc.semaphore(name)))

    def need(self, e, tok):
        if tok is None:
            return
        s, v = tok
        if e == "pe" and s is self.esem["pe"]:
            return
        if self.seen[e].get(s, 0) >= v:
            return
        self.q[e].append(("w", s, v))
        self.seen[e][s] = v

    def _deps(self, e, reads, writes):
        for r in reads:
            self.need(e, r.w)
        for w in writes:
            self.need(e, w.w)
            for s, v in w.r.items():
                self.need(e, (s, v))
            if w.rng is not None:
                lo, hi = w.rng
                for r2 in self.arena:
                    if r2 is not w and r2.rng[0] < hi and lo < r2.rng[1]:
                        self.need(e, r2.w)
                        for s, v in r2.r.items():
                            self.need(e, (s, v))

    @staticmethod
    def _commit(tok, reads, writes):
        s, v = tok
        for r in reads:
            if r.r.get(s, 0) < v:
                r.r[s] = v
        for w in writes:
            w.w = tok
            w.r = {}

    def op(self, e, fn, reads=(), writes=()):
        self._deps(e, reads, writes)
        s = self.esem[e]
        s.count += 1
        tok = (s, s.count)
        self.q[e].append(("i", fn, s, 1))
        self._commit(tok, reads, writes)
        return tok

    def mm(self, fns, reads=(), writes=()):
        self._deps("pe", reads, writes)
        s = self.esem["pe"]
        s.count += 1
        tok = (s, s.count)
        for fn in fns[:-1]:
            self.q["pe"].append(("i", fn, None, 0))
        self.q["pe"].append(("i", fns[-1], s, 1))
        self._commit(tok, reads, writes)
        return tok

    def dma(self, e, fn, sem, reads=(), writes=(), track=True):
        if track:
            self._deps(e, reads, writes)
        sem.count += 16
        tok = (sem, sem.count)
        self.q[e].append(("i", fn, sem, 16))
        if track:
            self._commit(tok, reads, writes)
        return tok

    def replay(self, e, eng):
        for it in self.q[e]:
            if it[0] == "w":
                eng.wait_ge(it[1].h, it[2])
            else:
                ins = it[1](eng)
                if it[2] is not None:
                    ins.then_inc(it[2].h, it[3])


class Cfg:
    def __init__(self, U=2048, NU=4):
        self.U = U
        self.NU = NU
        self.TU = U + 2 * HALO
        o = 0
        self.o_c = o; o += KC * NSEG
        self.o_adab = o; o += 2 * 72
        self.o_ng = o; o += 2 * 3 * 8
        self.o_cw = o; o += 2 * 3 * 4
        self.o_sgg = o; o += 2 * 4
        self.o_grp = o; o += 2 * 8
        self.o_fg = o; o += 8
        self.o_mask = o; o += NU * 2
        self.NV = o


def build_program(cfg):
    U, NU, TU, NV = cfg.U, cfg.NU, cfg.TU, cfg.NV
    nc = bass.Bass("TRN2", target_bir_lowering=False)
    dt_in = lambda name, shape: nc.dram_tensor(name, shape, F32, kind="ExternalInput").ap()
    xT = dt_in("xT", [NU, 128, KC, TU])
    vecs_d = dt_in("vecs", [128, NV])
    bsb_d = dt_in("bsb", [128, 2 * 4 * 128])
    wsT_d = dt_in("wsT", [2, 128, 8 * 128])
    ada_w = dt_in("ada_w", [2, D, 9 * D])
    ffn_w1 = dt_in("ffn_w1", [2, 2, D, 2 * FF])
    ffn_w2 = dt_in("ffn_w2", [2, 2, FF, D])
    w_in = dt_in("mix_w_in", [2, D, DIN])
    w_out = dt_in("mix_w_out", [2, D, D])
    yT = nc.dram_tensor("yT", [NU, 128, KC, U], F32, kind="ExternalOutput").ap()

    scr = lambda name, shape: nc.dram_tensor(name, shape, BF16, kind="Internal").ap()
    S1 = [[scr(f"S1_{l}_{f}", [JC, 128, KC * 256]) for f in range(2)] for l in range(2)]
    S2 = [[scr(f"S2_{l}_{f}", [8, 128, JC * 128]) for f in range(2)] for l in range(2)]
    SIN = [scr(f"SIN_{l}", [16, 128, KC * 128]) for l in range(2)]
    SINV = [scr(f"SINV_{l}", [128, KC * 512]) for l in range(2)]
    SOUT = [scr(f"SOUT_{l}", [8, 128, KC * 128]) for l in range(2)]
    SWS = scr("SWS", [2, 128, 8 * 128])

    off = [16640]

    def alloc(name, shape, dtype, at=None):
        esz = 2 if dtype == BF16 else 4
        n = esz
        for s in shape[1:]:
            n *= s
        if at is None:
            o = off[0]
            off[0] += (n + 63) // 64 * 64
        else:
            o = at
        return nc.alloc_sbuf_tensor_at(name, list(shape), dtype, offset=o), o, n

    xs, _, _ = alloc("xs", [128, KC, TU], F32)
    tT, o_tT, _ = alloc("tT", [128, KC, NT], F32)
    sqT, o_sq, _ = alloc("sqT", [128, KC, NT], F32R)
    adaA, _, _ = alloc("adaA", [128, KC, 384], F32, at=o_tT)
    adaB, _, _ = alloc("adaB", [128, KC, 384], F32, at=o_sq)
    rstd, _, _ = alloc("rstd", [128, 2, NT], F32)
    sgb, _, _ = alloc("sgb", [128, 2, NT], F32)
    wring, _, _ = alloc("wring", [128, NSLOT, SLOT_ELEMS], BF16)
    vecs, _, _ = alloc("vecs", [128, NV], F32)
    bsb, _, _ = alloc("bsb", [128, 2, 4, 128], F32)
    wsTb, _, _ = alloc("wsTb", [128, 2, 8, 128], BF16)
    ones, _, _ = alloc("ones", [128, 128], F32R)
    ones32, _, _ = alloc("ones32", [128, 128], F32)
    epsc, _, _ = alloc("epsc", [128, 4], F32)
    hcarry, _, _ = alloc("hcarry", [128, 4, 1], F32)
    sc, _, _ = alloc("sc", [128, KC, NSEG], F32)
    modsb, _, _ = alloc("modsb", [128, 2, 9, KC, NSEG], F32)
    Asc, _, _ = alloc("Asc", [128, 2, 3, KC, NSEG], F32)
    Gsc, _, _ = alloc("Gsc", [128, 2, 3, KC, NSEG], F32)
    gs32, _, _ = alloc("gs32", [128, 2, 3, KC], F32)
    grps, _, _ = alloc("grps", [128, 2, 8], F32)
    fgs, _, _ = alloc("fgs", [128, 8], F32)
    st6, _, _ = alloc("st6", [128, 2, 6], F32)
    mv, _, _ = alloc("mv", [128, 2, 2], F32)
    rsv, _, _ = alloc("rsv", [128, 2, 1], F32)
    NTM = 258
    mu2, _, _ = alloc("mu2", [128, 2, KC, NTM], BF16)
    ub, o_ub, n_ub = alloc("ub", [128, KC, TSUB], BF16)
    hb, o_hb, n_hb = alloc("hb", [128, JC, TSUB], BF16)
    mo = [o_ub]
    mrange = {}

    def malloc(name, shape, dtype):
        t, o, n = alloc(name, shape, dtype, at=mo[0])
        mrange[name] = (o, o + n)
        mo[0] += (n + 63) // 64 * 64
        return t

    cbuf2 = [malloc(f"cbuf{i}", [128, 4, NTM], F32) for i in range(2)]
    hh2 = [malloc(f"hh{i}", [128, 4, NTM], F32) for i in range(2)]
    bbuf2 = [malloc(f"bbuf{i}", [128, 4, NTM], F32) for i in range(2)]
    ugbuf2 = [malloc(f"ugbuf{i}", [128, 4, NTM], F32) for i in range(2)]
    vn2 = [malloc(f"vn{i}", [128, 2, 512], BF16) for i in range(2)]
    ybf2 = [malloc(f"ybf{i}", [128, 8, 256], BF16) for i in range(2)]
    vg = malloc("vg", [128, 2, 512], F32)
    tmpb = malloc("tmpb", [128, 2, 4, 128], F32)
    off[0] = max(off[0], mo[0])
    assert off[0] <= 229376, off[0]

    ps = nc.alloc_psum_tensor("ps", [128, 8, 512], F32)

    stack = ExitStack()
    em = Emitter(nc, stack)

    R = {}

    def res(key):
        r = R.get(key)
        if r is None:
            r = R[key] = Res(str(key))
        return r

    def xs_res(a, b):
        return [res(("xs", blk)) for blk in range(a // 128, (b - 1) // 128 + 1)]

    banks = [res(("bank", i)) for i in range(8)]
    bank_ctr = [0]

    def next_bank():
        i = bank_ctr[0] % 8
        bank_ctr[0] += 1
        return i, banks[i]

    slot_res = [res(("slot", i)) for i in range(NSLOT)]
    slot_sem = [em.new_sem(f"slot{i}") for i in range(NSLOT)]
    slot_ctr = [0]

    csem = em.new_sem("cload")
    xsem = [em.new_sem(f"x{b}") for b in range((TU + 127) // 128)]
    ysem = em.new_sem("ystore")
    adasem = [em.new_sem("adaA"), em.new_sem("adaB")]

    R_vecs = res("vecs")
    em.dma("sp", lambda e: e.dma_start(out=vecs[:, :], in_=vecs_d), csem, writes=[R_vecs])
    csem2 = em.new_sem("cload2")
    csem3 = em.new_sem("cload3")
    R_bsb = res("bsb")
    em.dma("sp", lambda e: e.dma_start(out=bsb[:, :, :, :].rearrange("p a b c -> p (a b c)"), in_=bsb_d),
           csem2, writes=[R_bsb])

    cast_res = {}

    def cast(key, pairs):
        sem = em.new_sem("cast_" + key)
        for (o_ap, i_ap) in pairs:
            em.dma("pool", (lambda e, o_ap=o_ap, i_ap=i_ap: e.dma_start(out=o_ap, in_=i_ap)), sem, track=False)
        r = res(("cast", key))
        r.w = (sem, sem.count)
        cast_res[key] = r

    def cast_layer(l):
        for f in range(2):
            w1v = ffn_w1[l, f].rearrange("(kc p) (h j c) -> j p kc h c", p=128, h=2, c=128)
            s1v = S1[l][f].rearrange("j p (kc h c) -> j p kc h c", kc=KC, h=2)
            cast(f"S1_{l}_{f}", [(s1v[j][:, :, h, :], w1v[j][:, :, h, :]) for j in range(JC) for h in range(2)])
            w2v = ffn_w2[l, f].rearrange("(j p) (m c) -> m p j c", p=128, c=128)
            s2v = S2[l][f].rearrange("m p (j c) -> m p j c", c=128)
            cast(f"S2_{l}_{f}", [(s2v[m], w2v[m]) for m in range(8)])
            if f == 0:
                winv = w_in[l][:, 0:2048].rearrange("(kc p) (oc c) -> oc p kc c", p=128, c=128)
                sinv = SIN[l].rearrange("oc p (kc c) -> oc p kc c", c=128)
                pairs = [(sinv[oc], winv[oc]) for oc in range(16)]
                wv = w_in[l][:, 2048:2560].rearrange("(kc p) c -> p kc c", p=128)
                svv = SINV[l].rearrange("p (kc c) -> p kc c", c=512)
                pairs += [(svv[:, kc:kc + 2, :], wv[:, kc:kc + 2, :]) for kc in range(0, KC, 2)]
                cast(f"SIN_{l}", pairs)
                wov = w_out[l].rearrange("(mc p) (m c) -> m p mc c", p=128, c=128)
                sov = SOUT[l].rearrange("m p (mc c) -> m p mc c", c=128)
                cast(f"SOUT_{l}", [(sov[m], wov[m]) for m in range(8)])

    cast("SWS", [(SWS[l], wsT_d[l]) for l in range(2)])
    cast_layer(0)
    cast_layer(1)

    R_ws = res("wsTb")
    em.dma("sp", lambda e: e.dma_start(out=wsTb[:, :, :, :].rearrange("p l h c -> p l (h c)"),
                                       in_=SWS.rearrange("l p x -> p l x")),
           csem3, reads=[cast_res["SWS"]], writes=[R_ws])

    R_ones = res("ones")
    em.op("dve", lambda e: e.memset(ones32[:, :], 1.0), writes=[R_ones])
    em.op("dve", lambda e: e.tensor_copy(out=ones[:, :], in_=ones32[:, :]), reads=[R_ones], writes=[R_ones])
    R_cv = res("cv")
    for i_, v_ in enumerate((D * EPS, 512 * EPS, EPS)):
        em.op("dve", (lambda e, i_=i_, v_=v_: e.memset(epsc[:, i_:i_ + 1], v_)), writes=[R_cv])

    def vsl(o, n):
        return vecs[:, o:o + n]

    em.op("dve", lambda e: e.tensor_scalar(out=gs32[:, :, :, :].rearrange("p a b c -> p (a b c)"),
                                            in0=vsl(cfg.o_ng, 48), scalar1=32.0, scalar2=1.0, op0=ALU.mult, op1=ALU.mult),
          reads=[R_vecs], writes=[R_cv])
    em.op("dve", lambda e: e.tensor_scalar(out=grps[:, :, :].rearrange("p a b -> p (a b)"),
                                            in0=vsl(cfg.o_grp, 16), scalar1=math.sqrt(512.0), scalar2=1.0,
                                            op0=ALU.mult, op1=ALU.mult),
          reads=[R_vecs], writes=[R_cv])
    em.op("dve", lambda e: e.tensor_scalar(out=fgs[:, :], in0=vsl(cfg.o_fg, 8), scalar1=32.0, scalar2=1.0,
                                            op0=ALU.mult, op1=ALU.mult),
          reads=[R_vecs], writes=[R_cv])
    R_sc = res("sc")
    em.op("act", lambda e: e.activation(out=sc[:, :, :].rearrange("p a b -> p (a b)"),
                                         in_=vsl(cfg.o_c, KC * NSEG), func=AF.Silu),
          reads=[R_vecs], writes=[R_sc])

    R_mod = res("mod")
    R_t = res("tT")
    R_sqh = [res(("sqh", 0)), res(("sqh", 1))]
    ada_bufs = [(adaA, [R_t], adasem[0]), (adaB, R_sqh, adasem[1])]
    for l in range(2):
        bi, bres = next_bank()
        awv = ada_w[l].rearrange("(kc p) n -> p kc n", p=128)
        for g in range(24):
            buf, bufres, bsem = ada_bufs[g % 2]
            em.dma("sp", (lambda e, buf=buf, g=g, awv=awv: e.dma_start(out=buf[:, :, :],
                                                                         in_=awv[:, :, g * 384:(g + 1) * 384])),
                   bsem, writes=bufres)
            for o in range(3):
                oc = 3 * g + o
                fns = []
                for kc in range(KC):
                    fns.append(lambda e, buf=buf, o=o, kc=kc, oc=oc, bi=bi: e.matmul(
                        ps[:, bi, oc * NSEG:(oc + 1) * NSEG], buf[:, kc, o * 128:(o + 1) * 128],
                        sc[:, kc, :], start=(kc == 0), stop=(kc == KC - 1)))
                em.mm(fns, reads=bufres + [R_sc], writes=[bres])
        em.op("dve", (lambda e, l=l, bi=bi: e.tensor_tensor(
            out=modsb[:, l, :, :, :].rearrange("p a b c -> p (a b) c"),
            in0=ps[:, bi, 0:72 * NSEG].rearrange("p (a c) -> p a c", c=NSEG),
            in1=vecs[:, cfg.o_adab + 72 * l: cfg.o_adab + 72 * (l + 1)].unsqueeze(2).to_broadcast([128, 72, NSEG]),
            op=ALU.add)), reads=[bres, R_vecs], writes=[R_mod])
        for k in range(3):
            em.op("dve", (lambda e, l=l, k=k: e.scalar_tensor_tensor(
                out=Asc[:, l, k, :, :], in0=modsb[:, l, 3 * k + 1, :, :], scalar=1.0,
                in1=gs32[:, l, k, :].unsqueeze(2).to_broadcast([128, KC, NSEG]),
                op0=ALU.add, op1=ALU.mult)), reads=[R_mod, R_cv], writes=[R_mod])
            em.op("dve", (lambda e, l=l, k=k: e.tensor_scalar(out=Gsc[:, l, k, :, :], in0=modsb[:, l, 3 * k + 2, :, :],
                scalar1=(1.0 if k == 1 else 0.5), scalar2=1.0, op0=ALU.mult, op1=ALU.mult)),
                reads=[R_mod], writes=[R_mod])

    R_rstd = [res(("rstd", i)) for i in range(2)]
    rstd_ctr = [0]
    R_sgb = [res(("sgb", i)) for i in range(2)]
    sgb_ctr = [0]

    def load_w(view_out_fn, in_ap, cres):
        i = slot_ctr[0] % NSLOT
        slot_ctr[0] += 1
        em.dma("sp", (lambda e, i=i: e.dma_start(out=view_out_fn(wring[:, i, :]), in_=in_ap)),
               slot_sem[i], reads=[cres], writes=[slot_res[i]])
        return i, slot_res[i]

    def sq_res(c0, nch):
        return [R_sqh[h] for h in range(2) if c0 < 4 * (h + 1) and 4 * h < c0 + nch]

    def rms_sq(src_ap_fn, nch, N, src_res, c0=0):
        em.op("act", (lambda e: e.activation(out=sqT[:, c0:c0 + nch, 0:N], in_=src_ap_fn(), func=AF.Square)),
              reads=src_res, writes=sq_res(c0, nch))

    def rms_fin(nch, N, eps_scaled, c0=0):
        bi, bres = next_bank()
        fns = [(lambda e, c=c, bi=bi: e.matmul(ps[:, bi, 0:N], ones[:, :], sqT[:, c0 + c, 0:N],
                                               start=(c == 0), stop=(c == nch - 1))) for c in range(nch)]
        em.mm(fns, reads=sq_res(c0, nch) + [R_ones], writes=[bres])
        ri = rstd_ctr[0] % 2
        rstd_ctr[0] += 1
        ec = {D * EPS: 0, 512 * EPS: 1, EPS: 2}[eps_scaled]
        em.op("act", (lambda e, bi=bi, ri=ri: e.activation(out=rstd[:, ri, 0:N], in_=ps[:, bi, 0:N], func=AF.Ln,
                                                            bias=epsc[:, ec:ec + 1], scale=1.0)),
              reads=[bres, R_cv], writes=[R_rstd[ri]])
        em.op("act", (lambda e, ri=ri: e.activation(out=rstd[:, ri, 0:N], in_=rstd[:, ri, 0:N], func=AF.Exp,
                                                     scale=-0.5)),
              reads=[R_rstd[ri]], writes=[R_rstd[ri]])
        return ri

    def rms_stats(src_ap_fn, nch, N, src_res, eps_scaled):
        rms_sq(src_ap_fn, nch, N, src_res, 0)
        return rms_fin(nch, N, eps_scaled, 0)

    def norm_mod(l, k, seg, a, b, dst, doff, dres, t_eng="dve"):
        N = b - a
        xr = xs_res(a, b)
        ri = rms_stats(lambda: xs[:, :, a:b], KC, N, xr, D * EPS)
        em.op(t_eng, (lambda e: e.tensor_tensor(
            out=tT[:, :, 0:N], in0=xs[:, :, a:b],
            in1=rstd[:, ri, 0:N].unsqueeze(1).to_broadcast([128, KC, N]), op=ALU.mult)),
            reads=xr + [R_rstd[ri]], writes=[R_t])
        for kc in range(KC):
            em.op("act", (lambda e, kc=kc: e.activation(
                out=dst[:, kc, doff:doff + N], in_=tT[:, kc, 0:N], func=AF.Identity,
                bias=modsb[:, l, 3 * k, kc, seg:seg + 1], scale=Asc[:, l, k, kc, seg:seg + 1])),
                reads=[R_t, R_mod], writes=[dres])

    items = []

    def ares(key, lo, hi):
        r = R.get(key)
        if r is None:
            r = R[key] = Res(str(key), (lo, hi))
            em.arena.append(r)
        return r

    def ub_res(n):
        return ares(("ub", n), o_ub, o_ub + n_ub)

    def hb_res(j, n):
        return ares(("hb", j, n), o_hb + j * TSUB * 2, o_hb + (j + 1) * TSUB * 2)

    def ffn(l, f, seg, lo, hi, fin_u=None):
        k = 0 if f == 0 else 2
        for (sa, sb) in split_even(lo, hi, TSUB):
            items.append(("A", lambda sa=sa, sb=sb: ffn_A(l, k, seg, sa, sb), (sa, sb)))
            items.append(("B", lambda sa=sa, sb=sb: ffn_B(l, f, sa, sb), None))
            items.append(("C", lambda sa=sa, sb=sb: ffn_C(l, f, k, seg, sa, sb), (sa, sb),
                          None if fin_u is None else (fin_u, sa, sb)))

    def ffn_A(l, k, seg, sa, sb):
        for n, (a, b) in enumerate(split_even(sa, sb, NT)):
            norm_mod(l, k, seg, a, b, ub, a - sa, ub_res(n))

    def ffn_B(l, f, sa, sb):
        nts = split_even(sa, sb, NT)
        for jj in range(0, JC, 2):
            si, sres = load_w(lambda s: s.rearrange("p (a x) -> p a x", a=2),
                              S1[l][f][jj:jj + 2].rearrange("a p x -> p a x"), cast_res[f"S1_{l}_{f}"])
            for jl in range(2):
                j = jj + jl
                for n, (a, b) in enumerate(nts):
                    N = b - a
                    o = a - sa
                    bg, bgres = next_bank()
                    fns = [(lambda e, kc=kc, bg=bg, jl=jl, o=o, N=N, si=si: e.matmul(
                        ps[:, bg, 0:N], wring[:, si, :].rearrange("p (a k c) -> p a k c", a=2, k=KC)[:, jl, kc, 0:128],
                        ub[:, kc, o:o + N], start=(kc == 0), stop=(kc == KC - 1))) for kc in range(KC)]
                    em.mm(fns, reads=[sres, ub_res(n)], writes=[bgres])
                    bu, bures = next_bank()
                    fns = [(lambda e, kc=kc, bu=bu, jl=jl, o=o, N=N, si=si: e.matmul(
                        ps[:, bu, 0:N], wring[:, si, :].rearrange("p (a k c) -> p a k c", a=2, k=KC)[:, jl, kc, 128:256],
                        ub[:, kc, o:o + N], start=(kc == 0), stop=(kc == KC - 1))) for kc in range(KC)]
                    em.mm(fns, reads=[sres, ub_res(n)], writes=[bures])
                    gi = sgb_ctr[0] % 2
                    sgb_ctr[0] += 1
                    em.op("act", (lambda e, bg=bg, gi=gi, N=N: e.activation(
                        out=sgb[:, gi, 0:N], in_=ps[:, bg, 0:N], func=AF.Silu)),
                        reads=[bgres], writes=[R_sgb[gi]])
                    em.op("dve", (lambda e, bu=bu, gi=gi, N=N, j=j, o=o: e.tensor_tensor(
                        out=hb[:, j, o:o + N], in0=ps[:, bu, 0:N], in1=sgb[:, gi, 0:N], op=ALU.mult)),
                        reads=[bures, R_sgb[gi]], writes=[hb_res(j, n)])

    def ffn_C(l, f, k, seg, sa, sb):
        nts = split_even(sa, sb, NT)
        for m in range(8):
            si, sres = load_w(lambda s: s[:, 0:JC * 128], S2[l][f][m], cast_res[f"S2_{l}_{f}"])
            for n, (a, b) in enumerate(nts):
                N = b - a
                o = a - sa
                bo, bores = next_bank()
                fns = [(lambda e, j=j, bo=bo, o=o, N=N, si=si: e.matmul(
                    ps[:, bo, 0:N], wring[:, si, j * 128:(j + 1) * 128], hb[:, j, o:o + N],
                    start=(j == 0), stop=(j == JC - 1))) for j in range(JC)]
                em.mm(fns, reads=[sres] + [hb_res(j, n) for j in range(JC)], writes=[bores])
                xr = xs_res(a, b)
                em.op("dve", (lambda e, bo=bo, N=N, m=m, a=a, b=b: e.scalar_tensor_tensor(
                    out=xs[:, m, a:b], in0=ps[:, bo, 0:N], scalar=Gsc[:, l, k, m, seg:seg + 1],
                    in1=xs[:, m, a:b], op0=ALU.mult, op1=ALU.add)),
                    reads=[bores, R_mod] + xr, writes=xr)

    mu_ctr = [0]
    R_mu = [res(("mu", i)) for i in range(2)]

    def mres(name, sub=None):
        lo, hi = mrange[name]
        if sub is not None:
            i, cnt = sub
            step = (hi - lo) // cnt
            lo, hi = lo + i * step, lo + (i + 1) * step
        return ares(("mix", name, sub), lo, hi)

    def mix(l, seg, base, nchunks, u):
        maxc = 2
        nsub = -(-nchunks // maxc)
        cb, cr = divmod(nchunks, nsub)
        c0 = 0
        subs = []
        for si_ in range(nsub):
            ncs = cb + (1 if si_ < cr else 0)
            a = base + 128 * c0
            c0 += ncs
            subs.append((a, a + 128 * ncs, ncs, {}, si_ == 0))

        def emitA(k):
            a, b, ncs, st, first = subs[k]
            items.append(("A", lambda: mix_A(l, seg, a, b, st), (a - 1, b + 1)))
            items.append(("B", lambda: mix_B1(l, seg, a, b, ncs, u, True, st), None))

        emitA(0)
        for k in range(nsub):
            a, b, ncs, st, first = subs[k]
            if k + 1 < nsub:
                a1, b1, ncs1, st1, first1 = subs[k + 1]
                items.append(("A", lambda a1=a1, b1=b1, st1=st1: mix_A(l, seg, a1, b1, st1), (a1, b1 + 1)))
            items.append(("B", lambda a=a, b=b, ncs=ncs, st=st: mix_B2a(l, seg, a, b, ncs, st), None))
            if k + 1 < nsub:
                items.append(("B", lambda a1=a1, b1=b1, ncs1=ncs1, st1=st1: mix_B1(l, seg, a1, b1, ncs1, u, False, st1), None))
            items.append(("B", lambda a=a, b=b, ncs=ncs, st=st: mix_B2b(l, seg, a, b, ncs, st), None))
            items.append(("C", lambda a=a, b=b, st=st: mix_C(l, seg, a, b, st), (a, b)))

    def mix_A(l, seg, a, b, st):
        mi = mu_ctr[0] % 2
        mu_ctr[0] += 1
        st["mi"] = mi
        norm_mod(l, 1, seg, a - 1, b + 1, mu2[:, mi], 0, R_mu[mi], t_eng="pool")

    def mix_B1(l, seg, a, b, ncs, u, first, st):
        ML, MR = 128, HALO + U
        mi = st["mi"]
        mu = mu2[:, mi]
        r_mu = R_mu[mi]
        cbuf, hh, bbuf, ugbuf, vn = cbuf2[mi], hh2[mi], bbuf2[mi], ugbuf2[mi], vn2[mi]
        r_c, r_hh, r_b, r_ug = (mres(f"cbuf{mi}"), mres(f"hh{mi}"), mres(f"bbuf{mi}"), mres(f"ugbuf{mi}"))
        Tm = b - a
        W = Tm + 2
        wa, wb = a - 1, b + 1
        for g in (1, 2, 0, 3):
            si, sres = load_w(lambda s: s.rearrange("p (a x) -> p a x", a=4),
                              SIN[l][4 * g:4 * g + 4].rearrange("a p x -> p a x"), cast_res[f"SIN_{l}"])
            for ch in range(4):
                bi, bres = next_bank()
                fns = [(lambda e, kc=kc, bi=bi, ch=ch, si=si: e.matmul(
                    ps[:, bi, 0:W], wring[:, si, :].rearrange("p (a k c) -> p a k c", a=4, k=KC)[:, ch, kc, :],
                    mu[:, kc, 0:W], start=(kc == 0), stop=(kc == KC - 1))) for kc in range(KC)]
                em.mm(fns, reads=[sres, r_mu], writes=[bres])
                if g == 1:
                    em.op("act", (lambda e, bi=bi, ch=ch: e.activation(
                        out=cbuf[:, ch, 0:W], in_=ps[:, bi, 0:W], func=AF.Copy)),
                        reads=[bres], writes=[r_c])
                elif g == 2:
                    em.op("dve", (lambda e, bi=bi, ch=ch: e.tensor_tensor(
                        out=hh[:, ch, 0:W], in0=ps[:, bi, 0:W], in1=cbuf[:, ch, 0:W], op=ALU.mult)),
                        reads=[bres, r_c], writes=[r_hh])
                elif g == 0:
                    em.op("act", (lambda e, bi=bi, ch=ch: e.activation(
                        out=bbuf[:, ch, 0:W], in_=ps[:, bi, 0:W], func=AF.Copy)),
                        reads=[bres], writes=[r_b])
                else:
                    em.op("act", (lambda e, bi=bi, ch=ch: e.activation(
                        out=ugbuf[:, ch, 0:W], in_=ps[:, bi, 0:W], func=AF.Gelu)),
                        reads=[bres], writes=[r_ug])
            if g == 2:
                for (mt, mcol) in ((ML, cfg.o_mask + 2 * u), (MR, cfg.o_mask + 2 * u + 1)):
                    if wa <= mt < wb:
                        col = mt - wa
                        em.op("pool", (lambda e, col=col, mcol=mcol: e.tensor_tensor(
                            out=hh[:, :, col:col + 1], in0=hh[:, :, col:col + 1],
                            in1=vecs[:, mcol:mcol + 1].unsqueeze(1).to_broadcast([128, 4, 1]), op=ALU.mult)),
                            reads=[r_hh, R_vecs], writes=[r_hh])
                r_hc = res("hcarry")
                if not first:
                    em.op("pool", (lambda e: e.tensor_copy(out=hh[:, :, 0:1], in_=hcarry[:, :, :])),
                          reads=[r_hc, r_hh], writes=[r_hh])
                em.op("pool", (lambda e: e.tensor_copy(out=hcarry[:, :, :], in_=hh[:, :, Tm:Tm + 1])),
                      reads=[r_hh], writes=[r_hc])
        si, sres = load_w(lambda s: s, SINV[l], cast_res[f"SIN_{l}"])
        r_vn = mres(f"vn{mi}")
        for q in range(ncs):
            t0 = 1 + 128 * q
            bi, bres = next_bank()
            fns = [(lambda e, kc=kc, bi=bi, t0=t0, si=si: e.matmul(
                ps[:, bi, 0:512], mu[:, kc, t0:t0 + 128],
                wring[:, si, :].rearrange("p (k c) -> p k c", k=KC)[:, kc, :],
                start=(kc == 0), stop=(kc == KC - 1))) for kc in range(KC)]
            em.mm(fns, reads=[sres, r_mu], writes=[bres])
            vi = q % 2
            r_vg = mres("vg", (vi, 2))
            em.op("act", (lambda e, bi=bi, vi=vi: e.activation(
                out=vg[:, vi, :], in_=ps[:, bi, 0:512], func=AF.Gelu)), reads=[bres], writes=[r_vg])
            r_st = res(("st", vi))
            em.op("dve", (lambda e, vi=vi: e.bn_stats(out=st6[:, vi, :], in_=vg[:, vi, :])),
                  reads=[r_vg], writes=[r_st])
            em.op("dve", (lambda e, vi=vi: e.bn_aggr(out=mv[:, vi, :], in_=st6[:, vi, :])),
                  reads=[r_st], writes=[r_st])
        for q in range(ncs):
            vi = q % 2
            r_vg = mres("vg", (vi, 2))
            r_st = res(("st", vi))
            em.op("act", (lambda e, vi=vi: e.activation(
                out=rsv[:, vi, :], in_=mv[:, vi, 1:2], func=AF.Ln, bias=epsc[:, 2:3], scale=1.0)),
                reads=[r_st, R_cv], writes=[r_st])
            em.op("act", (lambda e, vi=vi: e.activation(
                out=rsv[:, vi, :], in_=rsv[:, vi, :], func=AF.Exp, scale=-0.5)),
                reads=[r_st], writes=[r_st])
            em.op("dve", (lambda e, vi=vi, q=q: e.tensor_scalar(
                out=vn[:, q, :], in0=vg[:, vi, :], scalar1=mv[:, vi, 0:1], scalar2=rsv[:, vi, 0:1],
                op0=ALU.subtract, op1=ALU.mult)), reads=[r_vg, r_st], writes=[r_vn])

    def mix_B2a(l, seg, a, b, ncs, st):
        mi = st["mi"]
        cbuf, hh, bbuf, ugbuf, vn = cbuf2[mi], hh2[mi], bbuf2[mi], ugbuf2[mi], vn2[mi]
        r_hh, r_ug = mres(f"hh{mi}"), mres(f"ugbuf{mi}")
        r_cc = [mres(f"cbuf{mi}", (ch, 4)) for ch in range(4)]
        r_bc = [mres(f"bbuf{mi}", (ch, 4)) for ch in range(4)]
        r_c, r_b = mres(f"cbuf{mi}"), mres(f"bbuf{mi}")
        r_vn = mres(f"vn{mi}")
        Tm = b - a
        for q in range(ncs):
            t0 = 1 + 128 * q
            ti = q % 2
            r_tmp = mres("tmpb", (ti, 2))
            bi2, bres2 = next_bank()
            fns = [(lambda e, h=h, bi2=bi2, q=q: e.matmul(
                ps[64 * (h % 2):64 * (h % 2) + 64, bi2, (h // 2) * 128:(h // 2) * 128 + 128],
                vn[:, q, h * 64:(h + 1) * 64], wsTb[:, l, h, :], start=True, stop=True)) for h in range(8)]
            em.mm(fns, reads=[r_vn, R_ws], writes=[bres2])
            for hp in range(4):
                em.op("dve", (lambda e, hp=hp, bi2=bi2, ti=ti: e.scalar_tensor_tensor(
                    out=tmpb[:, ti, hp, :], in0=ps[:, bi2, hp * 128:(hp + 1) * 128],
                    scalar=vecs[:, cfg.o_sgg + 4 * l + hp: cfg.o_sgg + 4 * l + hp + 1],
                    in1=bsb[:, l, hp, :], op0=ALU.mult, op1=ALU.add)),
                    reads=[bres2, R_vecs, R_bsb], writes=[r_tmp])
            em.op("pool", (lambda e, t0=t0, ti=ti: e.tensor_tensor(
                out=ugbuf[:, :, t0:t0 + 128], in0=tmpb[:, ti, :, :], in1=ugbuf[:, :, t0:t0 + 128], op=ALU.mult)),
                reads=[r_tmp, r_ug], writes=[r_ug])
        cw = lambda tap, ch: vecs[:, cfg.o_cw + (l * 3 + tap) * 4 + ch: cfg.o_cw + (l * 3 + tap) * 4 + ch + 1]
        for ch in range(4):
            em.op("dve", (lambda e, ch=ch: e.tensor_scalar(
                out=cbuf[:, ch, 0:Tm], in0=hh[:, ch, 1:1 + Tm], scalar1=cw(1, ch), scalar2=ones32[:, 0:1],
                op0=ALU.mult, op1=ALU.mult)), reads=[r_hh, R_vecs, r_c], writes=[r_cc[ch]])
            em.op("dve", (lambda e, ch=ch: e.scalar_tensor_tensor(
                out=cbuf[:, ch, 0:Tm], in0=hh[:, ch, 0:Tm], scalar=cw(0, ch), in1=cbuf[:, ch, 0:Tm],
                op0=ALU.mult, op1=ALU.add)), reads=[r_hh, R_vecs, r_cc[ch]], writes=[r_cc[ch]])
            em.op("dve", (lambda e, ch=ch: e.scalar_tensor_tensor(
                out=cbuf[:, ch, 0:Tm], in0=hh[:, ch, 2:2 + Tm], scalar=cw(2, ch), in1=cbuf[:, ch, 0:Tm],
                op0=ALU.mult, op1=ALU.add)), reads=[r_hh, R_vecs, r_cc[ch]], writes=[r_cc[ch]])
            em.op("pool", (lambda e, ch=ch: e.tensor_tensor(
                out=bbuf[:, ch, 1:1 + Tm], in0=cbuf[:, ch, 0:Tm], in1=bbuf[:, ch, 1:1 + Tm], op=ALU.mult)),
                reads=[r_cc[ch], r_b], writes=[r_bc[ch]])
        rms_sq(lambda: bbuf[:, :, 1:1 + Tm], 4, Tm, r_bc, 0)
        rms_sq(lambda: ugbuf[:, :, 1:1 + Tm], 4, Tm, [r_ug], 4)

    def mix_B2b(l, seg, a, b, ncs, st):
        mi = st["mi"]
        bbuf, ugbuf, ybf = bbuf2[mi], ugbuf2[mi], ybf2[mi]
        r_bc = [mres(f"bbuf{mi}", (ch, 4)) for ch in range(4)]
        r_ug, r_y = mres(f"ugbuf{mi}"), mres(f"ybf{mi}")
        Tm = b - a
        ri = rms_fin(4, Tm, 512 * EPS, 0)
        for ch in range(4):
            em.op("dve", (lambda e, ch=ch, ri=ri: e.scalar_tensor_tensor(
                out=ybf[:, ch, 0:Tm], in0=bbuf[:, ch, 1:1 + Tm], scalar=grps[:, l, ch:ch + 1],
                in1=rstd[:, ri, 0:Tm], op0=ALU.mult, op1=ALU.mult)),
                reads=[r_bc[ch], R_cv, R_rstd[ri]], writes=[r_y])
        ri = rms_fin(4, Tm, 512 * EPS, 4)
        for ch in range(4):
            em.op("dve", (lambda e, ch=ch, ri=ri: e.scalar_tensor_tensor(
                out=ybf[:, 4 + ch, 0:Tm], in0=ugbuf[:, ch, 1:1 + Tm], scalar=grps[:, l, 4 + ch:5 + ch],
                in1=rstd[:, ri, 0:Tm], op0=ALU.mult, op1=ALU.mult)),
                reads=[r_ug, R_cv, R_rstd[ri]], writes=[r_y])

    def mix_C(l, seg, a, b, st):
        mi = st["mi"]
        ybf = ybf2[mi]
        r_y = mres(f"ybf{mi}")
        Tm = b - a
        for mg in range(2):
            si, sres = load_w(lambda s: s.rearrange("p (a x) -> p a x", a=4),
                              SOUT[l][4 * mg:4 * mg + 4].rearrange("a p x -> p a x"), cast_res[f"SOUT_{l}"])
            for ml in range(4):
                m = 4 * mg + ml
                bi, bres = next_bank()
                fns = [(lambda e, mc=mc, bi=bi, ml=ml, si=si: e.matmul(
                    ps[:, bi, 0:Tm], wring[:, si, :].rearrange("p (a k c) -> p a k c", a=4, k=KC)[:, ml, mc, :],
                    ybf[:, mc, 0:Tm], start=(mc == 0), stop=(mc == KC - 1))) for mc in range(KC)]
                em.mm(fns, reads=[sres, r_y], writes=[bres])
                xr = xs_res(a, b)
                em.op("dve", (lambda e, bi=bi, m=m: e.scalar_tensor_tensor(
                    out=xs[:, m, a:b], in0=ps[:, bi, 0:Tm], scalar=Gsc[:, l, 1, m, seg:seg + 1],
                    in1=xs[:, m, a:b], op0=ALU.mult, op1=ALU.add)),
                    reads=[bres, R_mod] + xr, writes=xr)

    def final_norm(u, lo, hi):
        for (a, b) in split_even(lo, hi, NT):
            final_tile(u, a, b)

    def final_tile(u, a, b):
        N = b - a
        xr = xs_res(a, b)
        ri = rms_stats(lambda: xs[:, :, a:b], KC, N, xr, D * EPS)
        for kc in range(KC):
            em.op("dve", (lambda e, kc=kc: e.scalar_tensor_tensor(
                out=tT[:, kc, 0:N], in0=xs[:, kc, a:b], scalar=fgs[:, kc:kc + 1],
                in1=rstd[:, ri, 0:N], op0=ALU.mult, op1=ALU.mult)),
                reads=xr + [R_cv, R_rstd[ri]], writes=[R_t])
        em.dma("act", (lambda e: e.dma_start(out=yT[u, :, :, a - HALO:b - HALO], in_=tT[:, :, 0:N])),
               ysem, reads=[R_t], writes=[])

    nblk = (TU + 127) // 128

    def load_x(u, blks=None):
        for blk in (range(nblk) if blks is None else blks):
            a, b = blk * 128, min(TU, (blk + 1) * 128)
            em.dma("act", (lambda e, a=a, b=b, u=u: e.dma_start(out=xs[:, :, a:b], in_=xT[u, :, :, a:b])),
                   xsem[blk], writes=[res(("xs", blk))])

    for u in range(NU):
        seg = u // (NU // NSEG)
        if u == 0:
            items.append(("X", lambda u=u: load_x(u), None))
        ffn(0, 0, seg, 0, TU)
        mix(0, seg, 1, (TU - 2) // 128, u)
        ffn(0, 1, seg, 1, TU - 1)
        ffn(1, 0, seg, HALO - 1, HALO + U + 1)
        mix(1, seg, HALO, U // 128, u)
        ffn(1, 1, seg, HALO, HALO + U, fin_u=u)
    nh = 0
    for i in range(1, len(items)):
        if items[i][0] == "A" and items[i - 1][0] == "C":
            (ra, rb), (wa, wb) = items[i][2], items[i - 1][2]
            if rb <= wa or wb <= ra:
                items[i - 1], items[i] = items[i], items[i - 1]
                nh += 1
    print("hoisted", nh, "of", sum(1 for it in items if it[0] == "A"))
    out_items = []
    dead_lo = {}
    for it in items:
        out_items.append(it)
        if it[0] == "C" and len(it) > 3 and it[3] is not None:
            fu, fa, fb = it[3]
            out_items.append(("F", lambda fu=fu, fa=fa, fb=fb: final_norm(fu, fa, fb), None))
            if fu + 1 < NU:
                hi_blk = nblk if fb >= HALO + U else fb // 128
                lo_blk = dead_lo.get(fu, 0)
                if hi_blk > lo_blk:
                    out_items.append(("X", lambda fu=fu, lo_blk=lo_blk, hi_blk=hi_blk:
                                      load_x(fu + 1, range(lo_blk, hi_blk)), None))
                    dead_lo[fu] = hi_blk
    for it in out_items:
        it[1]()
    em.need("sp", (ysem, ysem.count))

    with nc.Block() as block:
        @block.tensor
        def _(eng):
            em.replay("pe", eng)

        @block.scalar
        def _(eng):
            em.replay("act", eng)

        @block.vector
        def _(eng):
            em.replay("dve", eng)

        @block.gpsimd
        def _(eng):
            em.replay("pool", eng)

        @block.sync
        def _(eng):
            em.replay("sp", eng)
    stack.close()
    counts = {e: sum(1 for it in em.q[e] if it[0] == "i") for e in em.ENGS}
    waits = {e: sum(1 for it in em.q[e] if it[0] == "w") for e in em.ENGS}
    print("instr counts", counts, "waits", waits, "sbuf bytes", off[0])
    return nc


def host_layout(cfg, x_prompt, x_sample, c_prompt, c_sample, ada_b, norm_g, conv_w, sg_norm_g, sg_ws,
                sg_bs, grp_norm_g, final_g):
    U, NU, TU = cfg.U, cfg.NU, cfg.TU
    SEG = U * (NU // NSEG)
    B, S, _ = x_prompt.shape
    DB, DS, _ = x_sample.shape
    assert S == SEG and B == N_CORES and DB * DS == N_CORES * SEG
    per = DS // SEG

    def fm(v):
        v = np.asarray(v, dtype=np.float32)
        sh = v.shape
        v = v.reshape(sh[:-1] + (sh[-1] // 128, 128))
        return np.moveaxis(v, -1, 0)

    common = np.zeros((128, cfg.NV), np.float32)
    common[:, cfg.o_adab:cfg.o_adab + 144] = fm(ada_b).reshape(128, 144)
    common[:, cfg.o_ng:cfg.o_ng + 48] = fm(norm_g).reshape(128, 48)
    common[:, cfg.o_cw:cfg.o_cw + 24] = fm(conv_w).reshape(128, 24)
    common[:, cfg.o_sgg:cfg.o_sgg + 8] = fm(sg_norm_g).reshape(128, 8)
    common[:, cfg.o_grp:cfg.o_grp + 16] = fm(grp_norm_g).reshape(128, 16)
    common[:, cfg.o_fg:cfg.o_fg + 8] = fm(final_g).reshape(128, 8)
    sb = np.asarray(sg_bs, np.float32).reshape(2, 4, 2, 1, 128)
    sb = np.broadcast_to(sb, (2, 4, 2, 64, 128))
    bsb = np.ascontiguousarray(sb.transpose(2, 3, 0, 1, 4).reshape(128, 2 * 4 * 128))
    wsT = np.ascontiguousarray(np.asarray(sg_ws, np.float32).transpose(0, 3, 1, 2).reshape(2, 128, 8 * 128))

    xp = np.asarray(x_prompt, np.float32)
    xsamp = np.asarray(x_sample, np.float32)
    maps = []
    for c in range(N_CORES):
        segs = [(xp[c], 0, S, np.asarray(c_prompt[c], np.float32)),
                (xsamp[c // per], (c % per) * SEG, DS, np.asarray(c_sample[c // per], np.float32))]
        xT = np.zeros((NU, 128, KC, TU), np.float32)
        vec = common.copy()
        for u in range(NU):
            sg = u // (NU // NSEG)
            seq, s0, slen, cvec = segs[sg]
            s = s0 + (u % (NU // NSEG)) * U
            lo, hi = s - HALO, s + U + HALO
            clo, chi = max(lo, 0), min(hi, slen)
            blk = seq[clo:chi]
            xT[u, :, :, clo - lo:chi - lo] = blk.reshape(-1, KC, 128).transpose(2, 1, 0)
            vec[:, cfg.o_mask + 2 * u] = 1.0 if lo >= 0 else 0.0
            vec[:, cfg.o_mask + 2 * u + 1] = 1.0 if hi <= slen else 0.0
        for sg in range(NSEG):
            for kc in range(KC):
                vec[:, cfg.o_c + kc * NSEG + sg] = segs[sg][3][kc * 128:(kc + 1) * 128]
        maps.append({"xT": xT, "vecs": vec, "bsb": bsb, "wsT": wsT})
    return maps


_CACHE = {}


def run(cfg, inputs, trace=False):
    f32 = lambda k: np.ascontiguousarray(np.asarray(inputs[k], dtype=np.float32))
    maps = host_layout(cfg, inputs["x_prompt"], inputs["x_sample"], inputs["c_prompt"], inputs["c_sample"],
                       inputs["ada_b"], inputs["norm_g"], inputs["conv_w"], inputs["sg_norm_g"],
                       inputs["sg_ws"], inputs["sg_bs"], inputs["grp_norm_g"], inputs["final_g"])
    shared = {k: f32(k) for k in ("ada_w", "ffn_w1", "ffn_w2", "mix_w_in", "mix_w_out")}
    for m in maps:
        m.update(shared)
    key = (cfg.U, cfg.NU)
    if key not in _CACHE:
        _CACHE[key] = build_program(cfg)
    nc = _CACHE[key]
    res = run_bass_kernel_spmd(nc, maps, core_ids=list(range(N_CORES)), **({"trace": True} if trace else {}))
    U, NU = cfg.U, cfg.NU
    SEG = U * (NU // NSEG)
    B, S, _ = inputs["x_prompt"].shape
    DB, DS, _ = inputs["x_sample"].shape
    per = DS // SEG
    y_prompt = np.empty((B, S, D), np.float32)
    y_sample = np.empty((DB, DS, D), np.float32)
    for c in range(N_CORES):
        yT = np.asarray(res.results[c]["yT"])
        y = yT.transpose(0, 3, 2, 1).reshape(NU * U, D)
        y_prompt[c] = y[:SEG]
        y_sample[c // per, (c % per) * SEG:(c % per + 1) * SEG] = y[SEG:]
    return (y_prompt, y_sample), res


def kernel(**inputs):
    cfg = Cfg(U=2048, NU=4)
    out, _ = run(cfg, inputs)
    return out
```

```python
import math
from contextlib import ExitStack

import numpy as np
import concourse.bass as bass
import concourse.mybir as mybir
from concourse.bass_utils import run_bass_kernel_spmd

F32 = mybir.dt.float32
F32R = mybir.dt.float32r
BF16 = mybir.dt.bfloat16
AF = mybir.ActivationFunctionType
ALU = mybir.AluOpType

D = 1024
KC = 8
FF = 2816
JC = 22
DIN = 2560
HALO = 129
EPS = 1e-6
NSEG = 2
NT = 386
TSUB = 776
NSLOT = 4
SLOT_ELEMS = 4096
N_CORES = 8


def split_even(a, b, maxw):
    n = b - a
    assert n % 2 == 0 and n > 0
    h = n // 2
    k = -(-n // maxw)
    base, rem = divmod(h, k)
    out = []
    s = a
    for i in range(k):
        w = 2 * (base + (1 if i < rem else 0))
        if w > 0:
            out.append((s, s + w))
        s += w
    assert s == b
    return out


class Sem:
    def __init__(self, handle):
        self.h = handle
        self.count = 0


class Res:
    __slots__ = ("w", "r", "name", "rng")

    def __init__(self, name="", rng=None):
        self.w = None
        self.r = {}
        self.name = name
        self.rng = rng


class Emitter:
    ENGS = ("pe", "act", "dve", "pool", "sp")

    def __init__(self, nc, stack):
        self.nc = nc
        self.stack = stack
        self.q = {e: [] for e in self.ENGS}
        self.seen = {e: {} for e in self.ENGS}
        self.esem = {e: self.new_sem("e_" + e) for e in ("pe", "act", "dve", "pool")}
        self.arena = []

    def new_sem(self, name):
        return Sem(self.stack.enter_context(self.nc.semaphore(name)))

    def need(self, e, tok):
        if tok is None:
            return
        s, v = tok
        if e == "pe" and s is self.esem["pe"]:
            return
        if self.seen[e].get(s, 0) >= v:
            return
        self.q[e].append(("w", s, v))
        self.seen[e][s] = v

    def _deps(self, e, reads, writes):
        for r in reads:
            self.need(e, r.w)
        for w in writes:
            self.need(e, w.w)
            for s, v in w.r.items():
                self.need(e, (s, v))
            if w.rng is not None:
                lo, hi = w.rng
                for r2 in self.arena:
                    if r2 is not w and r2.rng[0] < hi and lo < r2.rng[1]:
                        self.need(e, r2.w)
                        for s, v in r2.r.items():
                            self.need(e, (s, v))

    @staticmethod
    def _commit(tok, reads, writes):
        s, v = tok
        for r in reads:
            if r.r.get(s, 0) < v:
                r.r[s] = v
        for w in writes:
            w.w = tok
            w.r = {}

    def op(self, e, fn, reads=(), writes=()):
        self._deps(e, reads, writes)
        s = self.esem[e]
        s.count += 1
        tok = (s, s.count)
        self.q[e].append(("i", fn, s, 1))
        self._commit(tok, reads, writes)
        return tok

    def mm(self, fns, reads=(), writes=()):
        self._deps("pe", reads, writes)
        s = self.esem["pe"]
        s.count += 1
        tok = (s, s.count)
        for fn in fns[:-1]:
            self.q["pe"].append(("i", fn, None, 0))
        self.q["pe"].append(("i", fns[-1], s, 1))
        self._commit(tok, reads, writes)
        return tok

    def dma(self, e, fn, sem, reads=(), writes=(), track=True):
        if track:
            self._deps(e, reads, writes)
        sem.count += 16
        tok = (sem, sem.count)
        self.q[e].append(("i", fn, sem, 16))
        if track:
            self._commit(tok, reads, writes)
        return tok

    def replay(self, e, eng):
        for it in self.q[e]:
            if it[0] == "w":
                eng.wait_ge(it[1].h, it[2])
            else:
                ins = it[1](eng)
                if it[2] is not None:
                    ins.then_inc(it[2].h, it[3])


class Cfg:
    def __init__(self, U=2048, NU=4):
        self.U = U
        self.NU = NU
        self.TU = U + 2 * HALO
        o = 0
        self.o_c = o; o += KC * NSEG
        self.o_adab = o; o += 2 * 72
        self.o_ng = o; o += 2 * 3 * 8
        self.o_cw = o; o += 2 * 3 * 4
        self.o_sgg = o; o += 2 * 4
        self.o_grp = o; o += 2 * 8
        self.o_fg = o; o += 8
        self.o_mask = o; o += NU * 2
        self.NV = o


def build_program(cfg):
    U, NU, TU, NV = cfg.U, cfg.NU, cfg.TU, cfg.NV
    nc = bass.Bass("TRN2", target_bir_lowering=False)
    dt_in = lambda name, shape: nc.dram_tensor(name, shape, F32, kind="ExternalInput").ap()
    xT = dt_in("xT", [NU, 128, KC, TU])
    vecs_d = dt_in("vecs", [128, NV])
    bsb_d = dt_in("bsb", [128, 2 * 4 * 128])
    wsT_d = dt_in("wsT", [2, 128, 8 * 128])
    ada_w = dt_in("ada_w", [2, D, 9 * D])
    ffn_w1 = dt_in("ffn_w1", [2, 2, D, 2 * FF])
    ffn_w2 = dt_in("ffn_w2", [2, 2, FF, D])
    w_in = dt_in("mix_w_in", [2, D, DIN])
    w_out = dt_in("mix_w_out", [2, D, D])
    yT = nc.dram_tensor("yT", [NU, 128, KC, U], F32, kind="ExternalOutput").ap()

    scr = lambda name, shape: nc.dram_tensor(name, shape, BF16, kind="Internal").ap()
    S1 = [[scr(f"S1_{l}_{f}", [JC, 128, KC * 256]) for f in range(2)] for l in range(2)]
    S2 = [[scr(f"S2_{l}_{f}", [8, 128, JC * 128]) for f in range(2)] for l in range(2)]
    SIN = [scr(f"SIN_{l}", [16, 128, KC * 128]) for l in range(2)]
    SINV = [scr(f"SINV_{l}", [128, KC * 512]) for l in range(2)]
    SOUT = [scr(f"SOUT_{l}", [8, 128, KC * 128]) for l in range(2)]
    SWS = scr("SWS", [2, 128, 8 * 128])

    off = [16640]

    def alloc(name, shape, dtype, at=None):
        esz = 2 if dtype == BF16 else 4
        n = esz
        for s in shape[1:]:
            n *= s
        if at is None:
            o = off[0]
            off[0] += (n + 63) // 64 * 64
        else:
            o = at
        return nc.alloc_sbuf_tensor_at(name, list(shape), dtype, offset=o), o, n

    xs, _, _ = alloc("xs", [128, KC, TU], F32)
    tT, o_tT, _ = alloc("tT", [128, KC, NT], F32)
    sqT, o_sq, _ = alloc("sqT", [128, KC, NT], F32R)
    adaA, _, _ = alloc("adaA", [128, KC, 384], F32, at=o_tT)
    adaB, _, _ = alloc("adaB", [128, KC, 384], F32, at=o_sq)
    rstd, _, _ = alloc("rstd", [128, 2, NT], F32)
    sgb, _, _ = alloc("sgb", [128, 2, NT], F32)
    wring, _, _ = alloc("wring", [128, NSLOT, SLOT_ELEMS], BF16)
    vecs, _, _ = alloc("vecs", [128, NV], F32)
    bsb, _, _ = alloc("bsb", [128, 2, 4, 128], F32)
    wsTb, _, _ = alloc("wsTb", [128, 2, 8, 128], BF16)
    ones, _, _ = alloc("ones", [128, 128], F32R)
    ones32, _, _ = alloc("ones32", [128, 128], F32)
    epsc, _, _ = alloc("epsc", [128, 4], F32)
    hcarry, _, _ = alloc("hcarry", [128, 4, 1], F32)
    sc, _, _ = alloc("sc", [128, KC, NSEG], F32)
    modsb, _, _ = alloc("modsb", [128, 2, 9, KC, NSEG], F32)
    Asc, _, _ = alloc("Asc", [128, 2, 3, KC, NSEG], F32)
    Gsc, _, _ = alloc("Gsc", [128, 2, 3, KC, NSEG], F32)
    gs32, _, _ = alloc("gs32", [128, 2, 3, KC], F32)
    grps, _, _ = alloc("grps", [128, 2, 8], F32)
    fgs, _, _ = alloc("fgs", [128, 8], F32)
    st6, _, _ = alloc("st6", [128, 2, 6], F32)
    mv, _, _ = alloc("mv", [128, 2, 2], F32)
    rsv, _, _ = alloc("rsv", [128, 2, 1], F32)
    NTM = 258
    mu2, _, _ = alloc("mu2", [128, 2, KC, NTM], BF16)
    ub, o_ub, n_ub = alloc("ub", [128, KC, TSUB], BF16)
    hb, o_hb, n_hb = alloc("hb", [128, JC, TSUB], BF16)
    mo = [o_ub]
    mrange = {}

    def malloc(name, shape, dtype):
        t, o, n = alloc(name, shape, dtype, at=mo[0])
        mrange[name] = (o, o + n)
        mo[0] += (n + 63) // 64 * 64
        return t

    cbuf2 = [malloc(f"cbuf{i}", [128, 4, NTM], F32) for i in range(2)]
    hh2 = [malloc(f"hh{i}", [128, 4, NTM], F32) for i in range(2)]
    bbuf2 = [malloc(f"bbuf{i}", [128, 4, NTM], F32) for i in range(2)]
    ugbuf2 = [malloc(f"ugbuf{i}", [128, 4, NTM], F32) for i in range(2)]
    vn2 = [malloc(f"vn{i}", [128, 2, 512], BF16) for i in range(2)]
    ybf2 = [malloc(f"ybf{i}", [128, 8, 256], BF16) for i in range(2)]
    vg = malloc("vg", [128, 2, 512], F32)
    tmpb = malloc("tmpb", [128, 2, 4, 128], F32)
    off[0] = max(off[0], mo[0])
    assert off[0] <= 229376, off[0]

    ps = nc.alloc_psum_tensor("ps", [128, 8, 512], F32)

    stack = ExitStack()
    em = Emitter(nc, stack)

    R = {}

    def res(key):
        r = R.get(key)
        if r is None:
            r = R[key] = Res(str(key))
        return r

    def xs_res(a, b):
        return [res(("xs", blk)) for blk in range(a // 128, (b - 1) // 128 + 1)]

    banks = [res(("bank", i)) for i in range(8)]
    bank_ctr = [0]

    def next_bank():
        i = bank_ctr[0] % 8
        bank_ctr[0] += 1
        return i, banks[i]

    slot_res = [res(("slot", i)) for i in range(NSLOT)]
    slot_sem = [em.new_sem(f"slot{i}") for i in range(NSLOT)]
    slot_ctr = [0]

    csem = em.new_sem("cload")
    xsem = [em.new_sem(f"x{b}") for b in range((TU + 127) // 128)]
    ysem = em.new_sem("ystore")
    adasem = [em.new_sem("adaA"), em.new_sem("adaB")]

    R_vecs = res("vecs")
    em.dma("sp", lambda e: e.dma_start(out=vecs[:, :], in_=vecs_d), csem, writes=[R_vecs])
    csem2 = em.new_sem("cload2")
    csem3 = em.new_sem("cload3")
    R_bsb = res("bsb")
    em.dma("sp", lambda e: e.dma_start(out=bsb[:, :, :, :].rearrange("p a b c -> p (a b c)"), in_=bsb_d),
           csem2, writes=[R_bsb])

    cast_res = {}

    def cast(key, pairs):
        sem = em.new_sem("cast_" + key)
        for (o_ap, i_ap) in pairs:
            em.dma("pool", (lambda e, o_ap=o_ap, i_ap=i_ap: e.dma_start(out=o_ap, in_=i_ap)), sem, track=False)
        r = res(("cast", key))
        r.w = (sem, sem.count)
        cast_res[key] = r

    def cast_layer(l):
        for f in range(2):
            w1v = ffn_w1[l, f].rearrange("(kc p) (h j c) -> j p kc h c", p=128, h=2, c=128)
            s1v = S1[l][f].rearrange("j p (kc h c) -> j p kc h c", kc=KC, h=2)
            cast(f"S1_{l}_{f}", [(s1v[j][:, :, h, :], w1v[j][:, :, h, :]) for j in range(JC) for h in range(2)])
            w2v = ffn_w2[l, f].rearrange("(j p) (m c) -> m p j c", p=128, c=128)
            s2v = S2[l][f].rearrange("m p (j c) -> m p j c", c=128)
            cast(f"S2_{l}_{f}", [(s2v[m], w2v[m]) for m in range(8)])
            if f == 0:
                winv = w_in[l][:, 0:2048].rearrange("(kc p) (oc c) -> oc p kc c", p=128, c=128)
                sinv = SIN[l].rearrange("oc p (kc c) -> oc p kc c", c=128)
                pairs = [(sinv[oc], winv[oc]) for oc in range(16)]
                wv = w_in[l][:, 2048:2560].rearrange("(kc p) c -> p kc c", p=128)
                svv = SINV[l].rearrange("p (kc c) -> p kc c", c=512)
                pairs += [(svv[:, kc:kc + 2, :], wv[:, kc:kc + 2, :]) for kc in range(0, KC, 2)]
                cast(f"SIN_{l}", pairs)
                wov = w_out[l].rearrange("(mc p) (m c) -> m p mc c", p=128, c=128)
                sov = SOUT[l].rearrange("m p (mc c) -> m p mc c", c=128)
                cast(f"SOUT_{l}", [(sov[m], wov[m]) for m in range(8)])

    cast("SWS", [(SWS[l], wsT_d[l]) for l in range(2)])
    cast_layer(0)
    cast_layer(1)

    R_ws = res("wsTb")
    em.dma("sp", lambda e: e.dma_start(out=wsTb[:, :, :, :].rearrange("p l h c -> p l (h c)"),
                                       in_=SWS.rearrange("l p x -> p l x")),
           csem3, reads=[cast_res["SWS"]], writes=[R_ws])

    R_ones = res("ones")
    em.op("dve", lambda e: e.memset(ones32[:, :], 1.0), writes=[R_ones])
    em.op("dve", lambda e: e.tensor_copy(out=ones[:, :], in_=ones32[:, :]), reads=[R_ones], writes=[R_ones])
    R_cv = res("cv")
    for i_, v_ in enumerate((D * EPS, 512 * EPS, EPS)):
        em.op("dve", (lambda e, i_=i_, v_=v_: e.memset(epsc[:, i_:i_ + 1], v_)), writes=[R_cv])

    def vsl(o, n):
        return vecs[:, o:o + n]

    em.op("dve", lambda e: e.tensor_scalar(out=gs32[:, :, :, :].rearrange("p a b c -> p (a b c)"),
                                            in0=vsl(cfg.o_ng, 48), scalar1=32.0, scalar2=1.0, op0=ALU.mult, op1=ALU.mult),
          reads=[R_vecs], writes=[R_cv])
    em.op("dve", lambda e: e.tensor_scalar(out=grps[:, :, :].rearrange("p a b -> p (a b)"),
                                            in0=vsl(cfg.o_grp, 16), scalar1=math.sqrt(512.0), scalar2=1.0,
                                            op0=ALU.mult, op1=ALU.mult),
          reads=[R_vecs], writes=[R_cv])
    em.op("dve", lambda e: e.tensor_scalar(out=fgs[:, :], in0=vsl(cfg.o_fg, 8), scalar1=32.0, scalar2=1.0,
                                            op0=ALU.mult, op1=ALU.mult),
          reads=[R_vecs], writes=[R_cv])
    R_sc = res("sc")
    em.op("act", lambda e: e.activation(out=sc[:, :, :].rearrange("p a b -> p (a b)"),
                                         in_=vsl(cfg.o_c, KC * NSEG), func=AF.Silu),
          reads=[R_vecs], writes=[R_sc])

    R_mod = res("mod")
    R_t = res("tT")
    R_sqh = [res(("sqh", 0)), res(("sqh", 1))]
    ada_bufs = [(adaA, [R_t], adasem[0]), (adaB, R_sqh, adasem[1])]
    for l in range(2):
        bi, bres = next_bank()
        awv = ada_w[l].rearrange("(kc p) n -> p kc n", p=128)
        for g in range(24):
            buf, bufres, bsem = ada_bufs[g % 2]
            em.dma("sp", (lambda e, buf=buf, g=g, awv=awv: e.dma_start(out=buf[:, :, :],
                                                                         in_=awv[:, :, g * 384:(g + 1) * 384])),
                   bsem, writes=bufres)
            for o in range(3):
                oc = 3 * g + o
                fns = []
                for kc in range(KC):
                    fns.append(lambda e, buf=buf, o=o, kc=kc, oc=oc, bi=bi: e.matmul(
                        ps[:, bi, oc * NSEG:(oc + 1) * NSEG], buf[:, kc, o * 128:(o + 1) * 128],
                        sc[:, kc, :], start=(kc == 0), stop=(kc == KC - 1)))
                em.mm(fns, reads=bufres + [R_sc], writes=[bres])
        em.op("dve", (lambda e, l=l, bi=bi: e.tensor_tensor(
            out=modsb[:, l, :, :, :].rearrange("p a b c -> p (a b) c"),
            in0=ps[:, bi, 0:72 * NSEG].rearrange("p (a c) -> p a c", c=NSEG),
            in1=vecs[:, cfg.o_adab + 72 * l: cfg.o_adab + 72 * (l + 1)].unsqueeze(2).to_broadcast([128, 72, NSEG]),
            op=ALU.add)), reads=[bres, R_vecs], writes=[R_mod])
        for k in range(3):
            em.op("dve", (lambda e, l=l, k=k: e.scalar_tensor_tensor(
                out=Asc[:, l, k, :, :], in0=modsb[:, l, 3 * k + 1, :, :], scalar=1.0,
                in1=gs32[:, l, k, :].unsqueeze(2).to_broadcast([128, KC, NSEG]),
                op0=ALU.add, op1=ALU.mult)), reads=[R_mod, R_cv], writes=[R_mod])
            em.op("dve", (lambda e, l=l, k=k: e.tensor_scalar(out=Gsc[:, l, k, :, :], in0=modsb[:, l, 3 * k + 2, :, :],
                scalar1=(1.0 if k == 1 else 0.5), scalar2=1.0, op0=ALU.mult, op1=ALU.mult)),
                reads=[R_mod], writes=[R_mod])

    R_rstd = [res(("rstd", i)) for i in range(2)]
    rstd_ctr = [0]
    R_sgb = [res(("sgb", i)) for i in range(2)]
    sgb_ctr = [0]

    def load_w(view_out_fn, in_ap, cres):
        i = slot_ctr[0] % NSLOT
        slot_ctr[0] += 1
        em.dma("sp", (lambda e, i=i: e.dma_start(out=view_out_fn(wring[:, i, :]), in_=in_ap)),
               slot_sem[i], reads=[cres], writes=[slot_res[i]])
        return i, slot_res[i]

    def sq_res(c0, nch):
        return [R_sqh[h] for h in range(2) if c0 < 4 * (h + 1) and 4 * h < c0 + nch]

    def rms_sq(src_ap_fn, nch, N, src_res, c0=0):
        em.op("act", (lambda e: e.activation(out=sqT[:, c0:c0 + nch, 0:N], in_=src_ap_fn(), func=AF.Square)),
              reads=src_res, writes=sq_res(c0, nch))

    def rms_fin(nch, N, eps_scaled, c0=0):
        bi, bres = next_bank()
        fns = [(lambda e, c=c, bi=bi: e.matmul(ps[:, bi, 0:N], ones[:, :], sqT[:, c0 + c, 0:N],
                                               start=(c == 0), stop=(c == nch - 1))) for c in range(nch)]
        em.mm(fns, reads=sq_res(c0, nch) + [R_ones], writes=[bres])
        ri = rstd_ctr[0] % 2
        rstd_ctr[0] += 1
        ec = {D * EPS: 0, 512 * EPS: 1, EPS: 2}[eps_scaled]
        em.op("act", (lambda e, bi=bi, ri=ri: e.activation(out=rstd[:, ri, 0:N], in_=ps[:, bi, 0:N], func=AF.Ln,
                                                            bias=epsc[:, ec:ec + 1], scale=1.0)),
              reads=[bres, R_cv], writes=[R_rstd[ri]])
        em.op("act", (lambda e, ri=ri: e.activation(out=rstd[:, ri, 0:N], in_=rstd[:, ri, 0:N], func=AF.Exp,
                                                     scale=-0.5)),
              reads=[R_rstd[ri]], writes=[R_rstd[ri]])
        return ri

    def rms_stats(src_ap_fn, nch, N, src_res, eps_scaled):
        rms_sq(src_ap_fn, nch, N, src_res, 0)
        return rms_fin(nch, N, eps_scaled, 0)

    def norm_mod(l, k, seg, a, b, dst, doff, dres, t_eng="dve"):
        N = b - a
        xr = xs_res(a, b)
        ri = rms_stats(lambda: xs[:, :, a:b], KC, N, xr, D * EPS)
        em.op(t_eng, (lambda e: e.tensor_tensor(
            out=tT[:, :, 0:N], in0=xs[:, :, a:b],
            in1=rstd[:, ri, 0:N].unsqueeze(1).to_broadcast([128, KC, N]), op=ALU.mult)),
            reads=xr + [R_rstd[ri]], writes=[R_t])
        for kc in range(KC):
            em.op("act", (lambda e, kc=kc: e.activation(
                out=dst[:, kc, doff:doff + N], in_=tT[:, kc, 0:N], func=AF.Identity,
                bias=modsb[:, l, 3 * k, kc, seg:seg + 1], scale=Asc[:, l, k, kc, seg:seg + 1])),
                reads=[R_t, R_mod], writes=[dres])

    items = []

    def ares(key, lo, hi):
        r = R.get(key)
        if r is None:
            r = R[key] = Res(str(key), (lo, hi))
            em.arena.append(r)
        return r

    def ub_res(n):
        return ares(("ub", n), o_ub, o_ub + n_ub)

    def hb_res(j, n):
        return ares(("hb", j, n), o_hb + j * TSUB * 2, o_hb + (j + 1) * TSUB * 2)

    def ffn(l, f, seg, lo, hi, fin_u=None):
        k = 0 if f == 0 else 2
        for (sa, sb) in split_even(lo, hi, TSUB):
            items.append(("A", lambda sa=sa, sb=sb: ffn_A(l, k, seg, sa, sb), (sa, sb)))
            items.append(("B", lambda sa=sa, sb=sb: ffn_B(l, f, sa, sb), None))
            items.append(("C", lambda sa=sa, sb=sb: ffn_C(l, f, k, seg, sa, sb), (sa, sb),
                          None if fin_u is None else (fin_u, sa, sb)))

    def ffn_A(l, k, seg, sa, sb):
        for n, (a, b) in enumerate(split_even(sa, sb, NT)):
            norm_mod(l, k, seg, a, b, ub, a - sa, ub_res(n))

    def ffn_B(l, f, sa, sb):
        nts = split_even(sa, sb, NT)
        for jj in range(0, JC, 2):
            si, sres = load_w(lambda s: s.rearrange("p (a x) -> p a x", a=2),
                              S1[l][f][jj:jj + 2].rearrange("a p x -> p a x"), cast_res[f"S1_{l}_{f}"])
            for jl in range(2):
                j = jj + jl
                for n, (a, b) in enumerate(nts):
                    N = b - a
                    o = a - sa
                    bg, bgres = next_bank()
                    fns = [(lambda e, kc=kc, bg=bg, jl=jl, o=o, N=N, si=si: e.matmul(
                        ps[:, bg, 0:N], wring[:, si, :].rearrange("p (a k c) -> p a k c", a=2, k=KC)[:, jl, kc, 0:128],
                        ub[:, kc, o:o + N], start=(kc == 0), stop=(kc == KC - 1))) for kc in range(KC)]
                    em.mm(fns, reads=[sres, ub_res(n)], writes=[bgres])
                    bu, bures = next_bank()
                    fns = [(lambda e, kc=kc, bu=bu, jl=jl, o=o, N=N, si=si: e.matmul(
                        ps[:, bu, 0:N], wring[:, si, :].rearrange("p (a k c) -> p a k c", a=2, k=KC)[:, jl, kc, 128:256],
                        ub[:, kc, o:o + N], start=(kc == 0), stop=(kc == KC - 1))) for kc in range(KC)]
                    em.mm(fns, reads=[sres, ub_res(n)], writes=[bures])
                    gi = sgb_ctr[0] % 2
                    sgb_ctr[0] += 1
                    em.op("act", (lambda e, bg=bg, gi=gi, N=N: e.activation(
                        out=sgb[:, gi, 0:N], in_=ps[:, bg, 0:N], func=AF.Silu)),
                        reads=[bgres], writes=[R_sgb[gi]])
                    em.op("dve", (lambda e, bu=bu, gi=gi, N=N, j=j, o=o: e.tensor_tensor(
                        out=hb[:, j, o:o + N], in0=ps[:, bu, 0:N], in1=sgb[:, gi, 0:N], op=ALU.mult)),
                        reads=[bures, R_sgb[gi]], writes=[hb_res(j, n)])

    def ffn_C(l, f, k, seg, sa, sb):
        nts = split_even(sa, sb, NT)
        for m in range(8):
            si, sres = load_w(lambda s: s[:, 0:JC * 128], S2[l][f][m], cast_res[f"S2_{l}_{f}"])
            for n, (a, b) in enumerate(nts):
                N = b - a
                o = a - sa
                bo, bores = next_bank()
                fns = [(lambda e, j=j, bo=bo, o=o, N=N, si=si: e.matmul(
                    ps[:, bo, 0:N], wring[:, si, j * 128:(j + 1) * 128], hb[:, j, o:o + N],
                    start=(j == 0), stop=(j == JC - 1))) for j in range(JC)]
                em.mm(fns, reads=[sres] + [hb_res(j, n) for j in range(JC)], writes=[bores])
                xr = xs_res(a, b)
                em.op("dve", (lambda e, bo=bo, N=N, m=m, a=a, b=b: e.scalar_tensor_tensor(
                    out=xs[:, m, a:b], in0=ps[:, bo, 0:N], scalar=Gsc[:, l, k, m, seg:seg + 1],
                    in1=xs[:, m, a:b], op0=ALU.mult, op1=ALU.add)),
                    reads=[bores, R_mod] + xr, writes=xr)

    mu_ctr = [0]
    R_mu = [res(("mu", i)) for i in range(2)]

    def mres(name, sub=None):
        lo, hi = mrange[name]
        if sub is not None:
            i, cnt = sub
            step = (hi - lo) // cnt
            lo, hi = lo + i * step, lo + (i + 1) * step
        return ares(("mix", name, sub), lo, hi)

    def mix(l, seg, base, nchunks, u):
        maxc = 2
        nsub = -(-nchunks // maxc)
        cb, cr = divmod(nchunks, nsub)
        c0 = 0
        subs = []
        for si_ in range(nsub):
            ncs = cb + (1 if si_ < cr else 0)
            a = base + 128 * c0
            c0 += ncs
            subs.append((a, a + 128 * ncs, ncs, {}, si_ == 0))

        def emitA(k):
            a, b, ncs, st, first = subs[k]
            items.append(("A", lambda: mix_A(l, seg, a, b, st), (a - 1, b + 1)))
            items.append(("B", lambda: mix_B1(l, seg, a, b, ncs, u, True, st), None))

        emitA(0)
        for k in range(nsub):
            a, b, ncs, st, first = subs[k]
            if k + 1 < nsub:
                a1, b1, ncs1, st1, first1 = subs[k + 1]
                items.append(("A", lambda a1=a1, b1=b1, st1=st1: mix_A(l, seg, a1, b1, st1), (a1, b1 + 1)))
            items.append(("B", lambda a=a, b=b, ncs=ncs, st=st: mix_B2a(l, seg, a, b, ncs, st), None))
            if k + 1 < nsub:
                items.append(("B", lambda a1=a1, b1=b1, ncs1=ncs1, st1=st1: mix_B1(l, seg, a1, b1, ncs1, u, False, st1), None))
            items.append(("B", lambda a=a, b=b, ncs=ncs, st=st: mix_B2b(l, seg, a, b, ncs, st), None))
            items.append(("C", lambda a=a, b=b, st=st: mix_C(l, seg, a, b, st), (a, b)))

    def mix_A(l, seg, a, b, st):
        mi = mu_ctr[0] % 2
        mu_ctr[0] += 1
        st["mi"] = mi
        norm_mod(l, 1, seg, a - 1, b + 1, mu2[:, mi], 0, R_mu[mi], t_eng="dve")

    def mix_B1(l, seg, a, b, ncs, u, first, st):
        ML, MR = 128, HALO + U
        mi = st["mi"]
        mu = mu2[:, mi]
        r_mu = R_mu[mi]
        cbuf, hh, bbuf, ugbuf, vn = cbuf2[mi], hh2[mi], bbuf2[mi], ugbuf2[mi], vn2[mi]
        r_c, r_hh, r_b, r_ug = (mres(f"cbuf{mi}"), mres(f"hh{mi}"), mres(f"bbuf{mi}"), mres(f"ugbuf{mi}"))
        Tm = b - a
        W = Tm + 2
        wa, wb = a - 1, b + 1
        for g in (1, 2, 0, 3):
            si, sres = load_w(lambda s: s.rearrange("p (a x) -> p a x", a=4),
                              SIN[l][4 * g:4 * g + 4].rearrange("a p x -> p a x"), cast_res[f"SIN_{l}"])
            for ch in range(4):
                bi, bres = next_bank()
                fns = [(lambda e, kc=kc, bi=bi, ch=ch, si=si: e.matmul(
                    ps[:, bi, 0:W], wring[:, si, :].rearrange("p (a k c) -> p a k c", a=4, k=KC)[:, ch, kc, :],
                    mu[:, kc, 0:W], start=(kc == 0), stop=(kc == KC - 1))) for kc in range(KC)]
                em.mm(fns, reads=[sres, r_mu], writes=[bres])
                if g == 1:
                    em.op("act", (lambda e, bi=bi, ch=ch: e.activation(
                        out=cbuf[:, ch, 0:W], in_=ps[:, bi, 0:W], func=AF.Copy)),
                        reads=[bres], writes=[r_c])
                elif g == 2:
                    em.op("dve", (lambda e, bi=bi, ch=ch: e.tensor_tensor(
                        out=hh[:, ch, 0:W], in0=ps[:, bi, 0:W], in1=cbuf[:, ch, 0:W], op=ALU.mult)),
                        reads=[bres, r_c], writes=[r_hh])
                elif g == 0:
                    em.op("act", (lambda e, bi=bi, ch=ch: e.activation(
                        out=bbuf[:, ch, 0:W], in_=ps[:, bi, 0:W], func=AF.Copy)),
                        reads=[bres], writes=[r_b])
                else:
                    em.op("act", (lambda e, bi=bi, ch=ch: e.activation(
                        out=ugbuf[:, ch, 0:W], in_=ps[:, bi, 0:W], func=AF.Gelu)),
                        reads=[bres], writes=[r_ug])
            if g == 2:
                for (mt, mcol) in ((ML, cfg.o_mask + 2 * u), (MR, cfg.o_mask + 2 * u + 1)):
                    if wa <= mt < wb:
                        col = mt - wa
                        em.op("pool", (lambda e, col=col, mcol=mcol: e.tensor_tensor(
                            out=hh[:, :, col:col + 1], in0=hh[:, :, col:col + 1],
                            in1=vecs[:, mcol:mcol + 1].unsqueeze(1).to_broadcast([128, 4, 1]), op=ALU.mult)),
                            reads=[r_hh, R_vecs], writes=[r_hh])
                r_hc = res("hcarry")
                if not first:
                    em.op("pool", (lambda e: e.tensor_copy(out=hh[:, :, 0:1], in_=hcarry[:, :, :])),
                          reads=[r_hc, r_hh], writes=[r_hh])
                em.op("pool", (lambda e: e.tensor_copy(out=hcarry[:, :, :], in_=hh[:, :, Tm:Tm + 1])),
                      reads=[r_hh], writes=[r_hc])
        si, sres = load_w(lambda s: s, SINV[l], cast_res[f"SIN_{l}"])
        r_vn = mres(f"vn{mi}")
        for q in range(ncs):
            t0 = 1 + 128 * q
            bi, bres = next_bank()
            fns = [(lambda e, kc=kc, bi=bi, t0=t0, si=si: e.matmul(
                ps[:, bi, 0:512], mu[:, kc, t0:t0 + 128],
                wring[:, si, :].rearrange("p (k c) -> p k c", k=KC)[:, kc, :],
                start=(kc == 0), stop=(kc == KC - 1))) for kc in range(KC)]
            em.mm(fns, reads=[sres, r_mu], writes=[bres])
            vi = q % 2
            r_vg = mres("vg", (vi, 2))
            em.op("act", (lambda e, bi=bi, vi=vi: e.activation(
                out=vg[:, vi, :], in_=ps[:, bi, 0:512], func=AF.Gelu)), reads=[bres], writes=[r_vg])
            r_st = res(("st", vi))
            em.op("dve", (lambda e, vi=vi: e.bn_stats(out=st6[:, vi, :], in_=vg[:, vi, :])),
                  reads=[r_vg], writes=[r_st])
            em.op("dve", (lambda e, vi=vi: e.bn_aggr(out=mv[:, vi, :], in_=st6[:, vi, :])),
                  reads=[r_st], writes=[r_st])
        for q in range(ncs):
            vi = q % 2
            r_vg = mres("vg", (vi, 2))
            r_st = res(("st", vi))
            em.op("act", (lambda e, vi=vi: e.activation(
                out=rsv[:, vi, :], in_=mv[:, vi, 1:2], func=AF.Ln, bias=epsc[:, 2:3], scale=1.0)),
                reads=[r_st, R_cv], writes=[r_st])
            em.op("act", (lambda e, vi=vi: e.activation(
                out=rsv[:, vi, :], in_=rsv[:, vi, :], func=AF.Exp, scale=-0.5)),
                reads=[r_st], writes=[r_st])
            em.op("dve", (lambda e, vi=vi, q=q: e.tensor_scalar(
                out=vn[:, q, :], in0=vg[:, vi, :], scalar1=mv[:, vi, 0:1], scalar2=rsv[:, vi, 0:1],
                op0=ALU.subtract, op1=ALU.mult)), reads=[r_vg, r_st], writes=[r_vn])

    def mix_B2a(l, seg, a, b, ncs, st):
        mi = st["mi"]
        cbuf, hh, bbuf, ugbuf, vn = cbuf2[mi], hh2[mi], bbuf2[mi], ugbuf2[mi], vn2[mi]
        r_hh, r_ug = mres(f"hh{mi}"), mres(f"ugbuf{mi}")
        r_cc = [mres(f"cbuf{mi}", (ch, 4)) for ch in range(4)]
        r_bc = [mres(f"bbuf{mi}", (ch, 4)) for ch in range(4)]
        r_c, r_b = mres(f"cbuf{mi}"), mres(f"bbuf{mi}")
        r_vn = mres(f"vn{mi}")
        Tm = b - a
        for q in range(ncs):
            t0 = 1 + 128 * q
            ti = q % 2
            r_tmp = mres("tmpb", (ti, 2))
            bi2, bres2 = next_bank()
            fns = [(lambda e, h=h, bi2=bi2, q=q: e.matmul(
                ps[64 * (h % 2):64 * (h % 2) + 64, bi2, (h // 2) * 128:(h // 2) * 128 + 128],
                vn[:, q, h * 64:(h + 1) * 64], wsTb[:, l, h, :], start=True, stop=True)) for h in range(8)]
            em.mm(fns, reads=[r_vn, R_ws], writes=[bres2])
            for hp in range(4):
                em.op("dve", (lambda e, hp=hp, bi2=bi2, ti=ti: e.scalar_tensor_tensor(
                    out=tmpb[:, ti, hp, :], in0=ps[:, bi2, hp * 128:(hp + 1) * 128],
                    scalar=vecs[:, cfg.o_sgg + 4 * l + hp: cfg.o_sgg + 4 * l + hp + 1],
                    in1=bsb[:, l, hp, :], op0=ALU.mult, op1=ALU.add)),
                    reads=[bres2, R_vecs, R_bsb], writes=[r_tmp])
            em.op("pool", (lambda e, t0=t0, ti=ti: e.tensor_tensor(
                out=ugbuf[:, :, t0:t0 + 128], in0=tmpb[:, ti, :, :], in1=ugbuf[:, :, t0:t0 + 128], op=ALU.mult)),
                reads=[r_tmp, r_ug], writes=[r_ug])
        cw = lambda tap, ch: vecs[:, cfg.o_cw + (l * 3 + tap) * 4 + ch: cfg.o_cw + (l * 3 + tap) * 4 + ch + 1]
        for ch in range(4):
            em.op("dve", (lambda e, ch=ch: e.tensor_scalar(
                out=cbuf[:, ch, 0:Tm], in0=hh[:, ch, 1:1 + Tm], scalar1=cw(1, ch), scalar2=ones32[:, 0:1],
                op0=ALU.mult, op1=ALU.mult)), reads=[r_hh, R_vecs, r_c], writes=[r_cc[ch]])
            em.op("dve", (lambda e, ch=ch: e.scalar_tensor_tensor(
                out=cbuf[:, ch, 0:Tm], in0=hh[:, ch, 0:Tm], scalar=cw(0, ch), in1=cbuf[:, ch, 0:Tm],
                op0=ALU.mult, op1=ALU.add)), reads=[r_hh, R_vecs, r_cc[ch]], writes=[r_cc[ch]])
            em.op("dve", (lambda e, ch=ch: e.scalar_tensor_tensor(
                out=cbuf[:, ch, 0:Tm], in0=hh[:, ch, 2:2 + Tm], scalar=cw(2, ch), in1=cbuf[:, ch, 0:Tm],
                op0=ALU.mult, op1=ALU.add)), reads=[r_hh, R_vecs, r_cc[ch]], writes=[r_cc[ch]])
            em.op("pool", (lambda e, ch=ch: e.tensor_tensor(
                out=bbuf[:, ch, 1:1 + Tm], in0=cbuf[:, ch, 0:Tm], in1=bbuf[:, ch, 1:1 + Tm], op=ALU.mult)),
                reads=[r_cc[ch], r_b], writes=[r_bc[ch]])
        rms_sq(lambda: bbuf[:, :, 1:1 + Tm], 4, Tm, r_bc, 0)
        rms_sq(lambda: ugbuf[:, :, 1:1 + Tm], 4, Tm, [r_ug], 4)

    def mix_B2b(l, seg, a, b, ncs, st):
        mi = st["mi"]
        bbuf, ugbuf, ybf = bbuf2[mi], ugbuf2[mi], ybf2[mi]
        r_bc = [mres(f"bbuf{mi}", (ch, 4)) for ch in range(4)]
        r_ug, r_y = mres(f"ugbuf{mi}"), mres(f"ybf{mi}")
        Tm = b - a
        ri = rms_fin(4, Tm, 512 * EPS, 0)
        for ch in range(4):
            em.op("dve", (lambda e, ch=ch, ri=ri: e.scalar_tensor_tensor(
                out=ybf[:, ch, 0:Tm], in0=bbuf[:, ch, 1:1 + Tm], scalar=grps[:, l, ch:ch + 1],
                in1=rstd[:, ri, 0:Tm], op0=ALU.mult, op1=ALU.mult)),
                reads=[r_bc[ch], R_cv, R_rstd[ri]], writes=[r_y])
        ri = rms_fin(4, Tm, 512 * EPS, 4)
        for ch in range(4):
            em.op("dve", (lambda e, ch=ch, ri=ri: e.scalar_tensor_tensor(
                out=ybf[:, 4 + ch, 0:Tm], in0=ugbuf[:, ch, 1:1 + Tm], scalar=grps[:, l, 4 + ch:5 + ch],
                in1=rstd[:, ri, 0:Tm], op0=ALU.mult, op1=ALU.mult)),
                reads=[r_ug, R_cv, R_rstd[ri]], writes=[r_y])

    def mix_C(l, seg, a, b, st):
        mi = st["mi"]
        ybf = ybf2[mi]
        r_y = mres(f"ybf{mi}")
        Tm = b - a
        for mg in range(2):
            si, sres = load_w(lambda s: s.rearrange("p (a x) -> p a x", a=4),
                              SOUT[l][4 * mg:4 * mg + 4].rearrange("a p x -> p a x"), cast_res[f"SOUT_{l}"])
            for ml in range(4):
                m = 4 * mg + ml
                bi, bres = next_bank()
                fns = [(lambda e, mc=mc, bi=bi, ml=ml, si=si: e.matmul(
                    ps[:, bi, 0:Tm], wring[:, si, :].rearrange("p (a k c) -> p a k c", a=4, k=KC)[:, ml, mc, :],
                    ybf[:, mc, 0:Tm], start=(mc == 0), stop=(mc == KC - 1))) for mc in range(KC)]
                em.mm(fns, reads=[sres, r_y], writes=[bres])
                xr = xs_res(a, b)
                em.op("dve", (lambda e, bi=bi, m=m: e.scalar_tensor_tensor(
                    out=xs[:, m, a:b], in0=ps[:, bi, 0:Tm], scalar=Gsc[:, l, 1, m, seg:seg + 1],
                    in1=xs[:, m, a:b], op0=ALU.mult, op1=ALU.add)),
                    reads=[bres, R_mod] + xr, writes=xr)

    def final_norm(u, lo, hi):
        for (a, b) in split_even(lo, hi, NT):
            final_tile(u, a, b)

    def final_tile(u, a, b):
        N = b - a
        xr = xs_res(a, b)
        ri = rms_stats(lambda: xs[:, :, a:b], KC, N, xr, D * EPS)
        for kc in range(KC):
            em.op("dve", (lambda e, kc=kc: e.scalar_tensor_tensor(
                out=tT[:, kc, 0:N], in0=xs[:, kc, a:b], scalar=fgs[:, kc:kc + 1],
                in1=rstd[:, ri, 0:N], op0=ALU.mult, op1=ALU.mult)),
                reads=xr + [R_cv, R_rstd[ri]], writes=[R_t])
        em.dma("act", (lambda e: e.dma_start(out=yT[u, :, :, a - HALO:b - HALO], in_=tT[:, :, 0:N])),
               ysem, reads=[R_t], writes=[])

    nblk = (TU + 127) // 128

    def load_x(u, blks=None):
        for blk in (range(nblk) if blks is None else blks):
            a, b = blk * 128, min(TU, (blk + 1) * 128)
            em.dma("act", (lambda e, a=a, b=b, u=u: e.dma_start(out=xs[:, :, a:b], in_=xT[u, :, :, a:b])),
                   xsem[blk], writes=[res(("xs", blk))])

    for u in range(NU):
        seg = u // (NU // NSEG)
        if u == 0:
            items.append(("X", lambda u=u: load_x(u), None))
        ffn(0, 0, seg, 0, TU)
        mix(0, seg, 1, (TU - 2) // 128, u)
        ffn(0, 1, seg, 1, TU - 1)
        ffn(1, 0, seg, HALO - 1, HALO + U + 1)
        mix(1, seg, HALO, U // 128, u)
        ffn(1, 1, seg, HALO, HALO + U, fin_u=u)
    nh = 0
    for i in range(1, len(items)):
        if items[i][0] == "A" and items[i - 1][0] == "C":
            (ra, rb), (wa, wb) = items[i][2], items[i - 1][2]
            if rb <= wa or wb <= ra:
                items[i - 1], items[i] = items[i], items[i - 1]
                nh += 1
    print("hoisted", nh, "of", sum(1 for it in items if it[0] == "A"))
    out_items = []
    dead_lo = {}
    for it in items:
        out_items.append(it)
        if it[0] == "C" and len(it) > 3 and it[3] is not None:
            fu, fa, fb = it[3]
            out_items.append(("F", lambda fu=fu, fa=fa, fb=fb: final_norm(fu, fa, fb), None))
            if fu + 1 < NU:
                hi_blk = nblk if fb >= HALO + U else fb // 128
                lo_blk = dead_lo.get(fu, 0)
                if hi_blk > lo_blk:
                    out_items.append(("X", lambda fu=fu, lo_blk=lo_blk, hi_blk=hi_blk:
                                      load_x(fu + 1, range(lo_blk, hi_blk)), None))
                    dead_lo[fu] = hi_blk
    for it in out_items:
        it[1]()
    em.need("sp", (ysem, ysem.count))

    with nc.Block() as block:
        @block.tensor
        def _(eng):
            em.replay("pe", eng)

        @block.scalar
        def _(eng):
            em.replay("act", eng)

        @block.vector
        def _(eng):
            em.replay("dve", eng)

        @block.gpsimd
        def _(eng):
            em.replay("pool", eng)

        @block.sync
        def _(eng):
            em.replay("sp", eng)
    stack.close()
    counts = {e: sum(1 for it in em.q[e] if it[0] == "i") for e in em.ENGS}
    waits = {e: sum(1 for it in em.q[e] if it[0] == "w") for e in em.ENGS}
    print("instr counts", counts, "waits", waits, "sbuf bytes", off[0])
    return nc


def host_layout(cfg, x_prompt, x_sample, c_prompt, c_sample, ada_b, norm_g, conv_w, sg_norm_g, sg_ws,
                sg_bs, grp_norm_g, final_g):
    U, NU, TU = cfg.U, cfg.NU, cfg.TU
    SEG = U * (NU // NSEG)
    B, S, _ = x_prompt.shape
    DB, DS, _ = x_sample.shape
    assert S == SEG and B == N_CORES and DB * DS == N_CORES * SEG
    per = DS // SEG

    def fm(v):
        v = np.asarray(v, dtype=np.float32)
        sh = v.shape
        v = v.reshape(sh[:-1] + (sh[-1] // 128, 128))
        return np.moveaxis(v, -1, 0)

    common = np.zeros((128, cfg.NV), np.float32)
    common[:, cfg.o_adab:cfg.o_adab + 144] = fm(ada_b).reshape(128, 144)
    common[:, cfg.o_ng:cfg.o_ng + 48] = fm(norm_g).reshape(128, 48)
    common[:, cfg.o_cw:cfg.o_cw + 24] = fm(conv_w).reshape(128, 24)
    common[:, cfg.o_sgg:cfg.o_sgg + 8] = fm(sg_norm_g).reshape(128, 8)
    common[:, cfg.o_grp:cfg.o_grp + 16] = fm(grp_norm_g).reshape(128, 16)
    common[:, cfg.o_fg:cfg.o_fg + 8] = fm(final_g).reshape(128, 8)
    sb = np.asarray(sg_bs, np.float32).reshape(2, 4, 2, 1, 128)
    sb = np.broadcast_to(sb, (2, 4, 2, 64, 128))
    bsb = np.ascontiguousarray(sb.transpose(2, 3, 0, 1, 4).reshape(128, 2 * 4 * 128))
    wsT = np.ascontiguousarray(np.asarray(sg_ws, np.float32).transpose(0, 3, 1, 2).reshape(2, 128, 8 * 128))

    xp = np.asarray(x_prompt, np.float32)
    xsamp = np.asarray(x_sample, np.float32)
    maps = []
    for c in range(N_CORES):
        segs = [(xp[c], 0, S, np.asarray(c_prompt[c], np.float32)),
                (xsamp[c // per], (c % per) * SEG, DS, np.asarray(c_sample[c // per], np.float32))]
        xT = np.zeros((NU, 128, KC, TU), np.float32)
        vec = common.copy()
        for u in range(NU):
            sg = u // (NU // NSEG)
            seq, s0, slen, cvec = segs[sg]
            s = s0 + (u % (NU // NSEG)) * U
            lo, hi = s - HALO, s + U + HALO
            clo, chi = max(lo, 0), min(hi, slen)
            blk = seq[clo:chi]
            xT[u, :, :, clo - lo:chi - lo] = blk.reshape(-1, KC, 128).transpose(2, 1, 0)
            vec[:, cfg.o_mask + 2 * u] = 1.0 if lo >= 0 else 0.0
            vec[:, cfg.o_mask + 2 * u + 1] = 1.0 if hi <= slen else 0.0
        for sg in range(NSEG):
            for kc in range(KC):
                vec[:, cfg.o_c + kc * NSEG + sg] = segs[sg][3][kc * 128:(kc + 1) * 128]
        maps.append({"xT": xT, "vecs": vec, "bsb": bsb, "wsT": wsT})
    return maps


_CACHE = {}


def run(cfg, inputs, trace=False):
    f32 = lambda k: np.ascontiguousarray(np.asarray(inputs[k], dtype=np.float32))
    maps = host_layout(cfg, inputs["x_prompt"], inputs["x_sample"], inputs["c_prompt"], inputs["c_sample"],
                       inputs["ada_b"], inputs["norm_g"], inputs["conv_w"], inputs["sg_norm_g"],
                       inputs["sg_ws"], inputs["sg_bs"], inputs["grp_norm_g"], inputs["final_g"])
    shared = {k: f32(k) for k in ("ada_w", "ffn_w1", "ffn_w2", "mix_w_in", "mix_w_out")}
    for m in maps:
        m.update(shared)
    key = (cfg.U, cfg.NU)
    if key not in _CACHE:
        _CACHE[key] = build_program(cfg)
    nc = _CACHE[key]
    res = run_bass_kernel_spmd(nc, maps, core_ids=list(range(N_CORES)), **({"trace": True} if trace else {}))
    U, NU = cfg.U, cfg.NU
    SEG = U * (NU // NSEG)
    B, S, _ = inputs["x_prompt"].shape
    DB, DS, _ = inputs["x_sample"].shape
    per = DS // SEG
    y_prompt = np.empty((B, S, D), np.float32)
    y_sample = np.empty((DB, DS, D), np.float32)
    for c in range(N_CORES):
        yT = np.asarray(res.results[c]["yT"])
        y = yT.transpose(0, 3, 2, 1).reshape(NU * U, D)
        y_prompt[c] = y[:SEG]
        y_sample[c // per, (c % per) * SEG:(c % per + 1) * SEG] = y[SEG:]
    return (y_prompt, y_sample), res


def kernel(**inputs):
    cfg = Cfg(U=2048, NU=4)
    out, _ = run(cfg, inputs)
    return out
```
